# Optimizing a Trainium2 kernel written in Bass

```python
import jax, jax.numpy as jnp
from jax import lax
import numpy as np

D_MODEL = 2048
BATCH = 2
SEQ = 4096
DEPTH = 1

POOL_WIDTH = D_MODEL
POOL_WINDOWS = (2, 4, 8, 16)
N_POOL_GROUPS = len(POOL_WINDOWS)
POOL_GROUP_WIDTH = POOL_WIDTH // N_POOL_GROUPS
LRU_WIDTH = D_MODEL
LRU_BLOCK = 256
N_LRU_HEADS = LRU_WIDTH // LRU_BLOCK
CONV_WIDTH = 4
LRU_C = 8.0
N_DIRS = 2
N_BRANCHES = 2
D_FF = 4 * D_MODEL
IN_WIDTH = POOL_WIDTH + 2 * LRU_WIDTH + N_BRANCHES * D_MODEL
DN_ALPHA = (2.0 * DEPTH) ** 0.25
DN_BETA = (8.0 * DEPTH) ** -0.25
LN_EPS = 1e-5

kernel_name = "hybrid_pool_rglru_encoder_block"


def layer_norm(x, g, b):
    xf = x.astype(jnp.float32)
    mu = jnp.mean(xf, axis=-1, keepdims=True)
    xc = xf - mu
    var = jnp.mean(xc * xc, axis=-1, keepdims=True)
    y = xc * lax.rsqrt(var + LN_EPS) * g.astype(jnp.float32) + b.astype(jnp.float32)
    return y.astype(x.dtype)


def multiscale_pool(u, pool_w, pool_scale):
    B, S, P = u.shape
    uf = u.astype(jnp.float32)
    csum = jnp.pad(jnp.cumsum(uf, axis=1), ((0, 0), (1, 0), (0, 0)))
    t = jnp.arange(S)
    outs = []
    for g, w in enumerate(POOL_WINDOWS):
        lo = jnp.clip(t - w // 2, 0, S)
        hi = jnp.clip(t + w // 2, 0, S)
        sl = slice(g * POOL_GROUP_WIDTH, (g + 1) * POOL_GROUP_WIDTH)
        c = csum[:, :, sl]
        mean = (c[:, hi] - c[:, lo]) / (hi - lo).astype(jnp.float32)[None, :, None]
        outs.append(mean - uf[:, :, sl])
    d = jnp.stack(outs, axis=2)
    y = jnp.einsum('bsgi,gio->bsgo', d, pool_w.astype(jnp.float32)).reshape(B, S, P)
    return (y * pool_scale.astype(jnp.float32)).astype(u.dtype)


def centred_depthwise_conv(u, w, b):
    S = u.shape[1]
    left = CONV_WIDTH // 2
    right = CONV_WIDTH - 1 - left
    up = jnp.pad(u, ((0, 0), (left, right), (0, 0)))
    y = b
    for k in range(CONV_WIDTH):
        y = y + up[:, k:k + S, :] * w[k]
    return y


def _lin_combine(p, q):
    a1, b1 = p
    a2, b2 = q
    return a1 * a2, a2 * b1 + b2


def rg_lru(xc, wa, ba, wx, bx, lam, reverse):
    B, S, R = xc.shape
    xf = xc.astype(jnp.float32)
    xh = xf.reshape(B, S, N_LRU_HEADS, LRU_BLOCK)
    r = jax.nn.sigmoid(jnp.einsum('bshi,hio->bsho', xh, wa.astype(jnp.float32)).reshape(B, S, R) + ba.astype(jnp.float32))
    i = jax.nn.sigmoid(jnp.einsum('bshi,hio->bsho', xh, wx.astype(jnp.float32)).reshape(B, S, R) + bx.astype(jnp.float32))
    log_a = -LRU_C * jax.nn.softplus(-lam.astype(jnp.float32)) * r
    a = jnp.exp(log_a)
    inp = jnp.sqrt(-jnp.expm1(2.0 * log_a)) * (i * xf)
    _, h = lax.associative_scan(_lin_combine, (a, inp), axis=1, reverse=reverse)
    return h


def hybrid_mixer(x, w_in, pool_w, pool_scale, conv_w, conv_b, lru_wa, lru_ba, lru_wx, lru_bx,
                 lru_lambda, w_pool_up, w_lru_up, w_out, b_out):
    B, S, D = x.shape
    z = jnp.einsum('bsd,de->bse', x, w_in)
    o1 = POOL_WIDTH
    o2 = o1 + LRU_WIDTH
    o3 = o2 + LRU_WIDTH
    u_pool, u_lru, u_gate, g_logits = z[..., :o1], z[..., o1:o2], z[..., o2:o3], z[..., o3:]
    y_pool = multiscale_pool(u_pool, pool_w, pool_scale)
    xc = centred_depthwise_conv(u_lru, conv_w, conv_b)
    h = (rg_lru(xc, lru_wa[0], lru_ba[0], lru_wx[0], lru_bx[0], lru_lambda[0], False)
         + rg_lru(xc, lru_wa[1], lru_ba[1], lru_wx[1], lru_bx[1], lru_lambda[1], True))
    y_lru = h.astype(x.dtype) * jax.nn.gelu(u_gate)
    g = jax.nn.sigmoid(g_logits.astype(jnp.float32)).astype(x.dtype).reshape(B, S, N_BRANCHES, D)
    m = (g[:, :, 0] * jnp.einsum('bsp,pd->bsd', y_pool, w_pool_up)
         + g[:, :, 1] * jnp.einsum('bsr,rd->bsd', y_lru, w_lru_up))
    return jnp.einsum('bsd,de->bse', m, w_out) + b_out


def sq_relu_mlp(x, w1, b1, w2, b2):
    hdn = jnp.square(jax.nn.relu(jnp.einsum('bsd,df->bsf', x, w1) + b1))
    return jnp.einsum('bsf,fd->bsd', hdn, w2) + b2


def setup_inputs(seed: int = 0) -> dict:
    key = jax.random.key(seed)
    ks = jax.random.split(key, 24)
    f32 = jnp.float32
    L, D, P, R = DEPTH, D_MODEL, POOL_WIDTH, LRU_WIDTH
    nrm = lambda k, shape, s: jax.random.normal(k, shape, f32) * s
    a_base = jax.random.uniform(ks[10], (L, N_DIRS, R), f32, minval=0.9, maxval=0.999)
    s = a_base ** (1.0 / LRU_C)
    lam = jnp.log(s) - jnp.log1p(-s)
    return {
        "x": nrm(ks[0], (BATCH, SEQ, D), 1.0),
        "w_in": nrm(ks[1], (L, D, IN_WIDTH), D ** -0.5),
        "pool_w": nrm(ks[2], (L, N_POOL_GROUPS, POOL_GROUP_WIDTH, POOL_GROUP_WIDTH), POOL_GROUP_WIDTH ** -0.5),
        "pool_scale": 1.0 + nrm(ks[3], (L, P), 0.1),
        "conv_w": nrm(ks[4], (L, CONV_WIDTH, R), CONV_WIDTH ** -0.5),
        "conv_b": nrm(ks[5], (L, R), 0.01),
        "lru_wa": nrm(ks[6], (L, N_DIRS, N_LRU_HEADS, LRU_BLOCK, LRU_BLOCK), LRU_BLOCK ** -0.5),
        "lru_ba": nrm(ks[7], (L, N_DIRS, R), 0.01),
        "lru_wx": nrm(ks[8], (L, N_DIRS, N_LRU_HEADS, LRU_BLOCK, LRU_BLOCK), LRU_BLOCK ** -0.5),
        "lru_bx": nrm(ks[9], (L, N_DIRS, R), 0.01),
        "lru_lambda": lam,
        "w_pool_up": nrm(ks[11], (L, P, D), DN_BETA * P ** -0.5),
        "w_lru_up": nrm(ks[12], (L, R, D), DN_BETA * R ** -0.5),
        "w_out": nrm(ks[13], (L, D, D), DN_BETA * D ** -0.5),
        "b_out": nrm(ks[14], (L, D), 0.01),
        "ln1_g": 1.0 + nrm(ks[15], (L, D), 0.1),
        "ln1_b": nrm(ks[16], (L, D), 0.01),
        "w_ff1": nrm(ks[17], (L, D, D_FF), D ** -0.5),
        "b_ff1": nrm(ks[18], (L, D_FF), 0.01),
        "w_ff2": nrm(ks[19], (L, D_FF, D), DN_BETA * D_FF ** -0.5),
        "b_ff2": nrm(ks[20], (L, D), 0.01),
        "ln2_g": 1.0 + nrm(ks[21], (L, D), 0.1),
        "ln2_b": nrm(ks[22], (L, D), 0.01),
    }


def reference(x, w_in, pool_w, pool_scale, conv_w, conv_b, lru_wa, lru_ba, lru_wx, lru_bx,
              lru_lambda, w_pool_up, w_lru_up, w_out, b_out, ln1_g, ln1_b,
              w_ff1, b_ff1, w_ff2, b_ff2, ln2_g, ln2_b):
    for l in range(DEPTH):
        mix = hybrid_mixer(x, w_in[l], pool_w[l], pool_scale[l], conv_w[l], conv_b[l],
                           lru_wa[l], lru_ba[l], lru_wx[l], lru_bx[l], lru_lambda[l],
                           w_pool_up[l], w_lru_up[l], w_out[l], b_out[l])
        x = layer_norm(DN_ALPHA * x + mix, ln1_g[l], ln1_b[l])
        ff = sq_relu_mlp(x, w_ff1[l], b_ff1[l], w_ff2[l], b_ff2[l])
        x = layer_norm(DN_ALPHA * x + ff, ln2_g[l], ln2_b[l])
    return x
```

```python
import numpy as np
import concourse.bass as bass
import concourse.mybir as mybir
from concourse.bass_utils import run_bass_kernel_spmd

F32 = mybir.dt.float32
F32R = mybir.dt.float32r
AF = mybir.ActivationFunctionType
ALU = mybir.AluOpType
AX = mybir.AxisListType

D = 2048
S = 4096
NCORE = 8
T = 512
NBLK = 8
NIT = 7
TH = 528
TL = 516
ALPHA = 2.0 ** 0.25
EPS = 1e-5
CELL = 32

PS_, CW_, CB_, BA_, BX_, LAM_, BO_, G1_, B1_, BF1_, BF2_, G2_, B2_ = 0, 16, 80, 96, 128, 160, 192, 208, 224, 240, 304, 320, 336
NV = 352
HBA_, HBX_, CP_, HC_, HC512_, AG1_, AB1_ = 0, 32, 64, 96, 128, 160, 176
ND = 192
CWI_, BAI_, BXI_, LAMI_, KEEP_, SELFA_, SELBB_ = 0, 448, 560, 672, 784, 791, 798
NI = 808
HBAI_, HBXI_, CPI_, HCI_ = 0, 112, 224, 336
NDI = 448

ARENA = 50176
DEBUG = False
NTMP_SLOT = 528


class Base:
    def __init__(self, nc, a_off, off, n):
        self.nc, self.a_off, self.off, self.n = nc, a_off, off, n
        self._f = None
        self._r = None

    @property
    def F(self):
        if self._f is None:
            self._f = self.nc.alloc_sbuf_tensor_at(f"F{self.off}_{self.n}", [128, self.n], F32,
                                                   offset=self.a_off + self.off * 4)
        return self._f

    @property
    def R(self):
        if self._r is None:
            self._r = self.nc.alloc_sbuf_tensor_at(f"R{self.off}_{self.n}", [128, self.n], F32R,
                                                   offset=self.a_off + self.off * 4)
        return self._r


class V:
    def __init__(self, base, rel, n):
        self.base, self.rel, self.n = base, rel, n
        self.off = base.off + rel

    @property
    def f(self):
        return self.base.F[:, self.rel:self.rel + self.n]

    @property
    def r(self):
        return self.base.R[:, self.rel:self.rel + self.n]

    @property
    def rev(self):
        return self.base.F[:, self.rel:self.rel + self.n][:, ::-1]

    def sub(self, a, b):
        assert 0 <= a < b <= self.n
        return V(self.base, self.rel + a, b - a)

    def keys(self):
        return [("sb", c) for c in range(self.off // CELL, (self.off + self.n - 1) // CELL + 1)]


class Prog:
    ENG = ["pe", "act", "dve", "pool", "sp"]

    def __init__(self):
        self.q = {e: [] for e in self.ENG}
        self.cnt = {}
        self.waited = {e: {} for e in self.ENG}
        self.lw = {}
        self.rd = {}

    def op(self, eng, fn, reads=(), writes=(), sem=None):
        deps = {}

        def add(tok):
            if tok is None:
                return
            k, v = tok
            if deps.get(k, 0) < v:
                deps[k] = v
        for key in reads:
            add(self.lw.get(key))
        for key in writes:
            add(self.lw.get(key))
            for tok in self.rd.get(key, ()):
                add(tok)
        for k, v in deps.items():
            if k == eng and eng == "pe":
                continue
            if self.waited[eng].get(k, 0) >= v:
                continue
            self.waited[eng][k] = v
            self.q[eng].append(("wait", k, v))
        if sem is None:
            semk, inc = eng, 1
        else:
            semk, inc = sem, 16
        val = self.cnt.get(semk, 0) + inc
        self.cnt[semk] = val
        self.q[eng].append(("op", fn, semk, inc))
        tok = (semk, val)
        for key in writes:
            self.lw[key] = tok
            self.rd[key] = []
        for key in reads:
            self.rd.setdefault(key, []).append(tok)
        return tok

    def final_wait(self, eng, tok):
        self.q[eng].append(("wait", tok[0], tok[1]))


def keys_of(*items):
    out = []
    for it in items:
        if isinstance(it, V):
            out.extend(it.keys())
        elif isinstance(it, list):
            out.extend(it)
        else:
            out.append(it)
    return out


def build_program():
    nc = bass.Bass("TRN2", target_bir_lowering=False)
    xm = nc.dram_tensor("xm", [2, 128, 16 * TH], F32R, kind="ExternalInput").ap()
    xs = nc.dram_tensor("xs", [NIT, 128, 16 * TL], F32R, kind="ExternalInput").ap()
    lru_wi = nc.dram_tensor("lru_wi", [NIT * 8, 128, 1024], F32R, kind="ExternalInput").ap()
    itm_d = nc.dram_tensor("itm", [128, NI], F32, kind="ExternalInput").ap()
    w_in_t = nc.dram_tensor("w_in_t", [80, 128, 2048], F32R, kind="ExternalInput").ap()
    pool_w_t = nc.dram_tensor("pool_w_t", [4, 128, 2048], F32R, kind="ExternalInput").ap()
    lru_w_t = nc.dram_tensor("lru_w_t", [8, 128, 2048], F32R, kind="ExternalInput").ap()
    wpu_t = nc.dram_tensor("wpu_t", [16, 128, 2048], F32R, kind="ExternalInput").ap()
    wlu_t = nc.dram_tensor("wlu_t", [16, 128, 2048], F32R, kind="ExternalInput").ap()
    wout_t = nc.dram_tensor("wout_t", [16, 128, 2048], F32R, kind="ExternalInput").ap()
    w1_t = nc.dram_tensor("w1_t", [64, 128, 2048], F32R, kind="ExternalInput").ap()
    w2_t = nc.dram_tensor("w2_t", [64, 128, 2048], F32R, kind="ExternalInput").ap()
    cvec_d = nc.dram_tensor("cvec", [128, NV], F32, kind="ExternalInput").ap()
    rc_d = nc.dram_tensor("rc", [128, 128], F32, kind="ExternalInput").ap()
    msk_d = nc.dram_tensor("msk", [128, 16], F32, kind="ExternalInput").ap()
    outT = nc.dram_tensor("outT", [2, 128, 16 * T], F32, kind="ExternalOutput").ap()
    dbg = nc.dram_tensor("dbg", [8, 128, 16 * T], F32, kind="ExternalOutput").ap() if DEBUG else None

    P = Prog()

    import contextlib
    with contextlib.ExitStack() as es:
        cv = es.enter_context(nc.sbuf_tensor("cv", [128, NV], F32))
        dv = es.enter_context(nc.sbuf_tensor("dv", [128, ND], F32))
        rc = es.enter_context(nc.sbuf_tensor("rcs", [128, 128], F32))
        msk = es.enter_context(nc.sbuf_tensor("msks", [128, 16], F32))
        ones = es.enter_context(nc.sbuf_tensor("ones", [128, 128], F32))
        scs = es.enter_context(nc.sbuf_tensor("scs", [128, 64], F32))
        itm = es.enter_context(nc.sbuf_tensor("itms", [128, NI], F32))
        dvi = es.enter_context(nc.sbuf_tensor("dvi", [128, NDI], F32))
        EST = es.enter_context(nc.sbuf_tensor("EST", [128, NIT * 16], F32))
        INI = es.enter_context(nc.sbuf_tensor("INI", [128, NIT * 16], F32))
        CFO = es.enter_context(nc.sbuf_tensor("CFO", [128, 32], F32))
        CBO = es.enter_context(nc.sbuf_tensor("CBO", [128, 32], F32))
        sm = es.enter_context(nc.sbuf_tensor("sm", [128, 64], F32))
        smi = es.enter_context(nc.sbuf_tensor("smi", [128, 224], F32))
        ps = es.enter_context(nc.psum_tensor("ps", [128, 8, 512], F32))
        a_off = (nc.sbuf_base + 63) // 64 * 64
        assert a_off + ARENA * 4 <= nc.sbuf_top, (a_off, nc.sbuf_top)
        bases = {}

        def A(off, n):
            assert off + n <= ARENA
            if (off, n) not in bases:
                bases[(off, n)] = Base(nc, a_off, off, n)
            return V(bases[(off, n)], 0, n)


        bank_ctr = [0]
        sc_ctr = [0]

        def pb():
            b = bank_ctr[0] % 6
            bank_ctr[0] += 1
            return b
        aux_ctr = [0]

        def pb_aux():
            b = 6 + aux_ctr[0] % 2
            aux_ctr[0] += 1
            return b

        def PSK(b):
            return ("ps", b)

        def mm_group(bank, n, pairs, reads, f32r=True, n0=0):
            def fn(e):
                last = None
                L = len(pairs)
                for i, (l, r) in enumerate(pairs):
                    if f32r:
                        la, ra = l.r, r.r
                    else:
                        la = l.f if isinstance(l, V) else l
                        ra = r.f
                    last = e.matmul(ps[:, bank, n0:n0 + n], la, ra, start=(i == 0), stop=(i == L - 1))
                return last
            P.op("pe", fn, reads=keys_of(*reads), writes=[PSK(bank)])

        def act(out, in_, func, bias=None, scale=None, reads=(), writes=()):
            kw = {}
            if bias is not None:
                kw["bias"] = bias
            if scale is not None:
                kw["scale"] = scale
            P.op("act", lambda e: e.activation(out=out, in_=in_, func=func, **kw),
                 reads=keys_of(*reads), writes=keys_of(*writes))

        def dve(fn, reads=(), writes=()):
            P.op("dve", fn, reads=keys_of(*reads), writes=keys_of(*writes))

        class WStream:
            def __init__(self, base, nslot, name):
                self.base, self.nslot, self.name = base, nslot, name
                self.plan = []
                self.issued = 0
                self.next = 0

            def slot(self, i):
                return A(self.base + (i % self.nslot) * 2048, 2048)

            def issue_to(self, upto):
                while self.issued < min(upto, len(self.plan)):
                    i = self.issued
                    src = self.plan[i][1]
                    dst = self.slot(i)
                    P.op("pool", (lambda s, d: (lambda e: e.dma_start(out=d.r, in_=s)))(src, dst),
                         reads=[], writes=dst.keys(), sem=f"{self.name}{i % self.nslot}")
                    self.issued += 1

            def get(self, tag=None):
                i = self.next
                assert tag is None or self.plan[i][0] == tag, (i, tag, self.plan[i][0])
                self.next += 1
                self.issue_to(i + self.nslot)
                return self.slot(i)

        def tap(idx, fnreg):
            if not DEBUG:
                return
            for j in range(16):
                reg = fnreg(j)
                P.op("sp", lambda e, reg=reg, j=j: e.dma_start(out=dbg[idx][:, j * T:(j + 1) * T], in_=reg.f),
                     reads=reg.keys(), writes=[("dr", "dbg", idx, j)], sem=f"dbg{j}")

        P.op("sp", lambda e: e.dma_start(out=cv[:], in_=cvec_d), writes=[("t", "cv")], sem="ld0")
        P.op("sp", lambda e: e.dma_start(out=rc[:], in_=rc_d), writes=[("t", "rc")], sem="ld1")
        P.op("sp", lambda e: e.dma_start(out=msk[:], in_=msk_d), writes=[("t", "msk")], sem="ld2")
        dve(lambda e: e.memset(ones[:], 1.0), writes=[("t", "ones")])
        P.op("sp", lambda e: e.dma_start(out=itm[:], in_=itm_d), writes=[("t", "itm")], sem="ld3")
        ITK, DIK = ("t", "itm"), ("t", "dvi")
        dve(lambda e: e.tensor_scalar(dvi[:, HBAI_:HBAI_ + 224], itm[:, BAI_:BAI_ + 224], 0.5, None, ALU.mult),
            reads=[ITK], writes=[DIK])
        act(smi[:, 0:112], itm[:, LAMI_:LAMI_ + 112], AF.Exp, scale=-1.0, reads=[ITK], writes=[("t", "smi")])
        act(smi[:, 112:224], smi[:, 0:112], AF.Ln, bias=1.0, reads=[("t", "smi")], writes=[("t", "smi2")])
        dve(lambda e: e.tensor_scalar(dvi[:, CPI_:CPI_ + 112], smi[:, 112:224], -8.0, None, ALU.mult),
            reads=[("t", "smi2")], writes=[DIK])
        dve(lambda e: e.tensor_scalar(dvi[:, HCI_:HCI_ + 112], smi[:, 112:224], -4.0, None, ALU.mult),
            reads=[("t", "smi2")], writes=[DIK])
        CVK, DVK = ("t", "cv"), ("t", "dv")
        dve(lambda e: e.tensor_scalar(dv[:, HBA_:HBA_ + 64], cv[:, BA_:BA_ + 64], 0.5, None, ALU.mult),
            reads=[CVK], writes=[DVK])
        act(sm[:, 0:32], cv[:, LAM_:LAM_ + 32], AF.Exp, scale=-1.0, reads=[CVK], writes=[("t", "sm")])
        act(sm[:, 32:64], sm[:, 0:32], AF.Ln, bias=1.0, reads=[("t", "sm")], writes=[("t", "sm2")])
        dve(lambda e: e.tensor_scalar(dv[:, CP_:CP_ + 32], sm[:, 32:64], -8.0, None, ALU.mult),
            reads=[("t", "sm2")], writes=[DVK])
        dve(lambda e: e.tensor_scalar(dv[:, HC_:HC_ + 32], sm[:, 32:64], -4.0, None, ALU.mult),
            reads=[("t", "sm2")], writes=[DVK])
        dve(lambda e: e.tensor_scalar(dv[:, HC512_:HC512_ + 32], sm[:, 32:64], -4.0 * T, None, ALU.mult),
            reads=[("t", "sm2")], writes=[DVK])
        dve(lambda e: e.tensor_scalar(dv[:, AG1_:AG1_ + 16], cv[:, G1_:G1_ + 16], ALPHA, None, ALU.mult),
            reads=[CVK], writes=[DVK])
        dve(lambda e: e.scalar_tensor_tensor(dv[:, AB1_:AB1_ + 16], cv[:, B1_:B1_ + 16], ALPHA,
                                             cv[:, BF2_:BF2_ + 16], ALU.mult, ALU.add),
            reads=[CVK], writes=[DVK])

        def cvc(col):
            return cv[:, col:col + 1]

        def dvc(col):
            return dv[:, col:col + 1]

        class TPool:
            def __init__(self, regions):
                self.slots = []
                for (off, n) in regions:
                    k = n // NTMP_SLOT
                    for i in range(k):
                        self.slots.append(off + i * NTMP_SLOT)
                self.i = 0

            def get(self, n=T):
                off = self.slots[self.i % len(self.slots)]
                self.i += 1
                return A(off, n)

        def lru_head(h, xv, wnext, tp, dirs, cw, cb, outbox=None, pre=None, dbg_rnd=None, tpa=None, tph=None):
            tpa = tpa or tp
            if pre is not None:
                pre()
            xcs = []
            wu = [None, None]
            for ci in range(2):
                wu[ci] = wnext()
                bk = pb()
                mm_group(bk, 512, [(wu[ci].sub(k * 128, (k + 1) * 128), xv(k, 0, 512)) for k in range(16)],
                         reads=[wu[ci]] + [xv(k, 0, 512) for k in range(16)])
                bt = pb_aux()
                mm_group(bt, 4, [(wu[ci].sub(k * 128, (k + 1) * 128), xv(k, 512, 516)) for k in range(16)],
                         reads=[wu[ci]] + [xv(k, 512, 516) for k in range(16)])
                u = tpa.get(TL)
                act(u.sub(0, 512).f, ps[:, bk, 0:512], AF.Copy, reads=[PSK(bk)], writes=[u.sub(0, 512)])
                act(u.sub(512, 516).f, ps[:, bt, 0:4], AF.Copy, reads=[PSK(bt)], writes=[u.sub(512, 516)])
                ch = h * 2 + ci
                xc = tpa.get(T)
                (w0, wk0), (b0, bk0) = cw(0, ch), cb(ch)
                dve(lambda e, u=u, xc=xc, w0=w0, b0=b0: e.tensor_scalar(
                    xc.f, u.sub(0, 512).f, w0, b0, ALU.mult, ALU.add), reads=[u, wk0, bk0], writes=[xc])
                for k in (1, 2):
                    wk_, wkk = cw(k, ch)
                    dve(lambda e, u=u, xc=xc, k=k, wk_=wk_: e.scalar_tensor_tensor(
                        xc.f, u.sub(k, k + 512).f, wk_, xc.f, ALU.mult, ALU.add), reads=[u, xc, wkk], writes=[xc])
                w3, wk3 = cw(3, ch)
                dve(lambda e, u=u, xc=xc, w3=w3: e.scalar_tensor_tensor(
                    xc.r, u.sub(3, 515).f, w3, xc.f, ALU.mult, ALU.add), reads=[u, xc, wk3], writes=[xc])
                xcs.append(xc)
                if DEBUG and dbg_rnd == 0:
                    P.op("sp", lambda e, xc=xc, ch=ch: e.dma_start(out=dbg[6][:, ch * T:(ch + 1) * T], in_=xc.f),
                         reads=xc.keys(), writes=[("dr", "dbg", 6, ch)], sem=f"dbg{ch}")
            yield
            wg = wnext()
            items = []
            for di, dd in enumerate(dirs):
                for co in range(2):
                    ch = h * 2 + co
                    ba_, bx_ = pb(), pb()
                    for gate, bnk in ((0, ba_), (1, bx_)):
                        base = dd["gbase"](gate)
                        mm_group(bnk, 512,
                                 [(wg.sub(base + k * 256 + co * 128, base + k * 256 + co * 128 + 128), xcs[k])
                                  for k in range(2)], reads=[wg, xcs[0], xcs[1]])
                    t1, t2, t3 = tp.get(T), tp.get(T), tp.get(T)
                    (hba, k1), (hbx, k2), (hc, k3), (cp, k4) = dd["hba"](ch), dd["hbx"](ch), dd["hc"](ch), dd["cp"](ch)
                    act(t1.f, ps[:, ba_, :], AF.Tanh, bias=hba, scale=0.5, reads=[PSK(ba_), k1], writes=[t1])
                    act(t2.f, ps[:, bx_, :], AF.Tanh, bias=hbx, scale=0.5, reads=[PSK(bx_), k2], writes=[t2])
                    act(t3.f, t1.f, AF.Exp, bias=hc, scale=hc, reads=[t1, k3], writes=[t3])
                    act(t1.f, t1.f, AF.Exp, bias=cp, scale=cp, reads=[t1, k4], writes=[t1])
                    items.append((di, dd, co, ch, t1, t2, t3))
            yield
            res = {}
            for (di, dd, co, ch, t1, t2, t3) in items:
                act(t1.f, t1.f, AF.Sqrt, bias=1.0, scale=-1.0, reads=[t1], writes=[t1])
                dve(lambda e, t2=t2, xc=xcs[co]: e.scalar_tensor_tensor(
                    t2.f, t2.f, 1.0, xc.f, ALU.add, ALU.mult), reads=[t2, xcs[co]], writes=[t2])
                dve(lambda e, t2=t2, t1=t1: e.scalar_tensor_tensor(
                    t2.f, t2.f, 0.5, t1.f, ALU.mult, ALU.mult), reads=[t2, t1], writes=[t2])
                init, ikeys = dd["init"](ch)
                if not dd["rev"]:
                    dve(lambda e, t1=t1, t3=t3, t2=t2, init=init: e.tensor_tensor_scan(
                        t1.f, t3.f, t2.f, init, ALU.mult, ALU.add), reads=[t3, t2] + ikeys, writes=[t1])
                else:
                    dve(lambda e, t1=t1, t3=t3, t2=t2, init=init: e.tensor_tensor_scan(
                        t1.rev, t3.rev, t2.rev, init, ALU.mult, ALU.add), reads=[t3, t2] + ikeys, writes=[t1])
                if dd.get("on_end") is not None:
                    dd["on_end"](ch, t1)
                res[(di, co)] = t1
            if outbox is not None:
                outs = []
                for co in range(2):
                    hf, hb = res[(0, co)], res[(1, co)]
                    ho = tph.get(T) if tph is not None else hf
                    dve(lambda e, hf=hf, hb=hb, ho=ho: e.tensor_tensor(ho.f, hf.f, hb.f, ALU.add),
                        reads=[hf, hb], writes=[ho])
                    hf = ho
                    outs.append(hf)
                    if DEBUG and dbg_rnd == 0:
                        chh = h * 2 + co
                        P.op("sp", lambda e, hf=hf, chh=chh: e.dma_start(out=dbg[7][:, chh * T:(chh + 1) * T], in_=hf.f),
                             reads=hf.keys(), writes=[("dr", "dbg", 7, chh)], sem=f"dbg{chh}")
                outbox.extend(outs)

        def run_pipe2(gens, after_c):
            n = len(gens)

            def fin(g):
                for _ in g:
                    pass
            next(gens[0])
            if n > 1:
                next(gens[1])
            next(gens[0])
            for i in range(1, n):
                if i + 1 < n:
                    next(gens[i + 1])
                fin(gens[i - 1])
                after_c(i - 1)
                next(gens[i])
            fin(gens[n - 1])
            after_c(n - 1)

        def run_pipe3(gens, after_c):
            n = len(gens)

            def fin(g):
                for _ in g:
                    pass
            next(gens[0])
            next(gens[1])
            next(gens[0])
            for i in range(1, n):
                if i + 1 < n:
                    next(gens[i + 1])
                fin(gens[i - 1])
                next(gens[i])
                after_c(i - 1)
            fin(gens[n - 1])
            after_c(n - 1)

        def run_pipe(gens, after_c):
            n = len(gens)

            def fin(g):
                for _ in g:
                    pass
            next(gens[0])
            next(gens[0])
            for i in range(1, n):
                next(gens[i])
                fin(gens[i - 1])
                after_c(i - 1)
                next(gens[i])
            fin(gens[n - 1])
            after_c(n - 1)

        HPG = 2
        NHG = 8 // HPG
        L1_WL = 0
        L1_WG = 2 * HPG * 2 * 2048
        NWGI = 4
        L1_XS = L1_WG + NWGI * 1024
        L1_TMP = L1_XS + 2 * 16 * TL
        tp1 = TPool([(L1_TMP, ARENA - L1_TMP)])
        wgi_ctr = [0]

        def cw_item(it):
            return lambda k, ch: (itm[:, CWI_ + it * 64 + k * 16 + ch:CWI_ + it * 64 + k * 16 + ch + 1], ITK)

        def cb_main(ch):
            return (cvc(CB_ + ch), CVK)

        def wl_slot(hg, i):
            return A(L1_WL + (hg % 2) * (2 * HPG * 2048) + i * 2048, 2048)

        def load_wl(hg):
            for hh in range(HPG):
                h = hg * HPG + hh
                for ci in range(2):
                    dst = wl_slot(hg, hh * 2 + ci)
                    P.op("pool", (lambda s_, d: (lambda e: e.dma_start(out=d.r, in_=s_)))(w_in_t[16 + h * 2 + ci], dst),
                         writes=dst.keys(), sem=f"wl{hg % 2}_{hh * 2 + ci}")
        load_wl(0)
        gens = []
        gi = 0
        for hg in range(NHG):
            for it in range(NIT):
                xb = A(L1_XS + (gi % 2) * 16 * TL, 16 * TL)

                def load_xs(it=it, xb=xb, gi=gi):
                    P.op("pool", (lambda s_, d: (lambda e: e.dma_start(out=d.r, in_=s_)))(xs[it], xb),
                         writes=xb.keys(), sem=f"xs{gi % 2}")

                def xv(k, a, b, xb=xb):
                    return xb.sub(k * TL + a, k * TL + b)
                for hh in range(HPG):
                    h = hg * HPG + hh
                    slot = wgi_ctr[0] % NWGI
                    wgi_ctr[0] += 1
                    wgi = A(L1_WG + slot * 1024, 1024)
                    nxt = hg + 1 if (hh == 0 and it == NIT - 2 and hg + 1 < NHG) else None

                    def pre(it=it, h=h, wgi=wgi, slot=slot, first=(hh == 0), lx=load_xs, nxt=nxt):
                        if first:
                            lx()
                        P.op("pool", (lambda s_, d: (lambda e: e.dma_start(out=d.r, in_=s_)))(lru_wi[it * 8 + h], wgi),
                             writes=wgi.keys(), sem=f"wgi{slot}")
                        if nxt is not None:
                            load_wl(nxt)

                    def mk_dir(it=it):
                        def col(base):
                            return lambda ch: (dvi[:, base + it * 16 + ch:base + it * 16 + ch + 1], DIK)

                        def init(ch):
                            if it == 0:
                                return 0.0, []
                            return INI[:, it * 16 + ch:it * 16 + ch + 1], [("t", "INI", it, ch)]

                        def on_end(ch, t1):
                            dve(lambda e, t1=t1: e.tensor_copy(EST[:, it * 16 + ch:it * 16 + ch + 1], t1.sub(511, 512).f),
                                reads=[t1], writes=[("t", "EST", it, ch)])
                            if it + 1 < NIT:
                                dve(lambda e: e.tensor_scalar(
                                    INI[:, (it + 1) * 16 + ch:(it + 1) * 16 + ch + 1], EST[:, it * 16 + ch:it * 16 + ch + 1],
                                    itm[:, KEEP_ + it + 1:KEEP_ + it + 2], None, ALU.mult),
                                    reads=[("t", "EST", it, ch), ITK], writes=[("t", "INI", it + 1, ch)])
                        return dict(gbase=lambda gate: gate * 512, hba=col(HBAI_), hbx=col(HBXI_), hc=col(HCI_),
                                    cp=col(CPI_), rev=False, init=init, on_end=on_end)
                    lst = [wl_slot(hg, hh * 2), wl_slot(hg, hh * 2 + 1), wgi]
                    gens.append(lru_head(h, xv, (lambda lst=lst: lst.pop(0)), tp1, [mk_dir()], cw_item(it), cb_main, pre=pre))
                gi += 1
        run_pipe2(gens, lambda i: None)

        ESTALL = [("t", "EST", it, ch) for it in range(NIT) for ch in range(16)]
        dve(lambda e: e.memset(CFO[:], 0.0), reads=ESTALL, writes=[("t", "CO", 0, 0), ("t", "CO", 1, 0)])
        dve(lambda e: e.memset(CBO[:], 0.0), writes=[("t", "CO", 0, 1), ("t", "CO", 1, 1)])
        for it in range(NIT):
            dve(lambda e, it=it: e.scalar_tensor_tensor(
                CFO[:, 0:16], EST[:, it * 16:(it + 1) * 16], itm[:, SELFA_ + it:SELFA_ + it + 1],
                CFO[:, 0:16], ALU.mult, ALU.add), reads=[ITK, ("t", "CO", 0, 0)], writes=[("t", "CO", 0, 0)])
            dve(lambda e, it=it: e.scalar_tensor_tensor(
                CBO[:, 16:32], EST[:, it * 16:(it + 1) * 16], itm[:, SELBB_ + it:SELBB_ + it + 1],
                CBO[:, 16:32], ALU.mult, ALU.add), reads=[ITK, ("t", "CO", 1, 1)], writes=[("t", "CO", 1, 1)])
        dve(lambda e: e.tensor_copy(CBO[:, 0:16], EST[:, (NIT - 1) * 16:NIT * 16]),
            reads=[("t", "CO", 0, 1)], writes=[("t", "CO", 0, 1)])

        M_XT = 0
        M_B1 = 16 * TH
        M_B2 = M_B1 + 8192
        M_B3 = M_B2 + 8192
        M_WS = M_B3 + 8192
        NSLOT = 6
        M_TMP = M_WS + NSLOT * 2048
        LNR = A(M_TMP, T)
        LNM = A(M_TMP + NTMP_SLOT, T)
        M_TMP2 = M_TMP + 2 * NTMP_SLOT
        tp_s = TPool([(M_TMP2, ARENA - M_TMP2)])
        tp_b = TPool([(M_TMP2, ARENA - M_TMP2), (M_B3, 8192)])
        tp_m1a = TPool([(M_B1, 12 * NTMP_SLOT)])
        tp_hs = TPool([(M_TMP2, 4 * NTMP_SLOT)])
        tp_m1 = TPool([(M_TMP2 + 4 * NTMP_SLOT, ARENA - M_TMP2 - 4 * NTMP_SLOT), (M_B3, 8192),
                       (M_B1 + 12 * NTMP_SLOT, 8192 - 12 * NTMP_SLOT)])

        def B1(j):
            return A(M_B1 + j * 512, 512)

        def B2(j):
            return A(M_B2 + j * 512, 512)

        def B3(j):
            return A(M_B3 + j * 512, 512)

        last_out_tok = None
        for rnd in range(2):
            ws = WStream(M_WS, NSLOT, f"ws{rnd}_")
            plan = []

            def win(i):
                return (("win", i), w_in_t[i])
            plan += [win(16), win(17), win(18), win(19), (("lru", 0), lru_w_t[0])]
            for h in range(1, 8):
                if h + 1 < 8:
                    plan += [win(16 + 2 * (h + 1)), win(17 + 2 * (h + 1))]
                plan += [(("lru", h), lru_w_t[h]), win(32 + 2 * (h - 1)), win(33 + 2 * (h - 1))]
            plan += [win(32 + 14), win(33 + 14)]
            for g in range(4):
                plan += [win(g * 4 + ci) for ci in range(4)] + [(("pw", g), pool_w_t[g])]
            for j in range(16):
                plan += [win(48 + j), (("wpu", j), wpu_t[j]), win(64 + j), (("wlu", j), wlu_t[j])]
            for j in range(16):
                plan += [(("wout", j), wout_t[j])]
            for qg in range(4):
                plan += [(("w1", qg * 16 + fi), w1_t[qg * 16 + fi]) for fi in range(16)]
                plan += [(("w2", qg * 16 + j), w2_t[qg * 16 + j]) for j in range(16)]
            ws.plan = plan

            xt = A(M_XT, 16 * TH)
            P.op("pool", (lambda s, d: (lambda e: e.dma_start(out=d.r, in_=s)))(xm[rnd], xt),
                 writes=xt.keys(), sem="xt")

            def xmain(k, xt=xt):
                return xt.sub(k * TH + 8, k * TH + 520)

            def xv(k, a, b, xt=xt):
                return xt.sub(k * TH + 6 + a, k * TH + 6 + b)

            gens, boxes = [], []
            for h in range(8):
                tags = [("win", 16 + 2 * h), ("win", 17 + 2 * h), ("lru", h)]
                box = []
                boxes.append(box)
                def mk_dirs(rnd=rnd):
                    ds = []
                    for d in range(2):
                        def col(base, d=d):
                            return lambda ch: (dv[:, base + d * 16 + ch:base + d * 16 + ch + 1], DVK)

                        def init(ch, d=d):
                            src = CFO if d == 0 else CBO
                            return src[:, rnd * 16 + ch:rnd * 16 + ch + 1], [("t", "CO", rnd, d)]
                        on_end = None
                        if d == 0 and rnd == 0:
                            def on_end(ch, t1):
                                dve(lambda e, t1=t1: e.tensor_copy(CFO[:, 16 + ch:16 + ch + 1], t1.sub(511, 512).f),
                                    reads=[t1], writes=[("t", "CO", 1, 0)])
                        ds.append(dict(gbase=(lambda gate, d=d: ((d * 2 + gate) * 2) * 256), hba=col(HBA_), hbx=col(HBX_),
                                       hc=col(HC_), cp=col(CP_), rev=(d == 1), init=init, on_end=on_end))
                    return ds
                gens.append(lru_head(h, xv, (lambda tags=tags: ws.get(tags.pop(0))), tp_m1, mk_dirs(),
                                     (lambda k, ch: (cvc(CW_ + k * 16 + ch), CVK)), cb_main, outbox=box, dbg_rnd=rnd, tpa=tp_m1a, tph=tp_hs))

            def gelu_part(h):
                hs = boxes[h]
                st = []
                for ci in range(2):
                    bk = pb()
                    wgt = ws.get(("win", 32 + 2 * h + ci))
                    mm_group(bk, 512, [(wgt.sub(k * 128, (k + 1) * 128), xmain(k)) for k in range(16)],
                             reads=[wgt] + [xmain(k) for k in range(16)])
                    g1 = tp_m1.get(T)
                    act(g1.f, ps[:, bk, :], AF.Square, reads=[PSK(bk)], writes=[g1])
                    st.append((bk, g1))
                for (bk, g1) in st:
                    dve(lambda e, g1=g1: e.tensor_scalar(g1.f, g1.f, 0.044715, 1.0, ALU.mult, ALU.add),
                        reads=[g1], writes=[g1])
                    dve(lambda e, g1=g1, bk=bk: e.tensor_tensor(g1.f, g1.f, ps[:, bk, :], ALU.mult),
                        reads=[g1, PSK(bk)], writes=[g1])
                for (bk, g1) in st:
                    act(g1.f, g1.f, AF.Tanh, scale=0.7978845608028654, reads=[g1], writes=[g1])
                for ci, (bk, g1) in enumerate(st):
                    ch = h * 2 + ci
                    dve(lambda e, g1=g1, bk=bk: e.scalar_tensor_tensor(
                        g1.f, g1.f, 1.0, ps[:, bk, :], ALU.add, ALU.mult), reads=[g1, PSK(bk)], writes=[g1])
                    dve(lambda e, g1=g1, hsv=hs[ci], ch=ch: e.scalar_tensor_tensor(
                        B2(ch).r, hsv.f, 0.5, g1.f, ALU.mult, ALU.mult), reads=[g1, hs[ci]], writes=[B2(ch)])
            run_pipe3(gens, gelu_part)

            if rnd == 0:
                tap(0, B2)
            for g in range(4):
                w = 2 << g
                dgs = []
                for ci in range(4):
                    wsl = ws.get()
                    bk = pb()
                    mm_group(bk, 512, [(wsl.sub(k * 128, (k + 1) * 128), xt.sub(k * TH, k * TH + 512)) for k in range(16)],
                             reads=[wsl, xt])
                    bt = pb_aux()
                    mm_group(bt, 16, [(wsl.sub(k * 128, (k + 1) * 128), xt.sub(k * TH + 512, k * TH + 528)) for k in range(16)],
                             reads=[wsl, xt])
                    u = tp_b.get(TH)
                    act(u.sub(0, 512).f, ps[:, bk, 0:512], AF.Copy, reads=[PSK(bk)], writes=[u.sub(0, 512)])
                    act(u.sub(512, 528).f, ps[:, bt, 0:16], AF.Copy, reads=[PSK(bt)], writes=[u.sub(512, 528)])
                    sa, sb_ = tp_b.get(TH), tp_b.get(TH)
                    dve(lambda e, u=u, sa=sa: e.tensor_tensor(sa.sub(0, 527).f, u.sub(0, 527).f, u.sub(1, 528).f, ALU.add),
                        reads=[u], writes=[sa])
                    cur, shift = sa, 7
                    if g >= 1:
                        dve(lambda e, sa=sa, sb_=sb_: e.tensor_tensor(sb_.sub(0, 525).f, sa.sub(0, 525).f, sa.sub(2, 527).f, ALU.add),
                            reads=[sa], writes=[sb_])
                        cur, shift = sb_, 6
                    if g >= 2:
                        dve(lambda e, sa=sa, sb_=sb_: e.tensor_tensor(sa.sub(0, 521).f, sb_.sub(0, 521).f, sb_.sub(4, 525).f, ALU.add),
                            reads=[sb_], writes=[sa])
                        cur, shift = sa, 4
                    if g >= 3:
                        dve(lambda e, sa=sa, sb_=sb_: e.tensor_tensor(sb_.sub(0, 513).f, sa.sub(0, 513).f, sa.sub(8, 521).f, ALU.add),
                            reads=[sa], writes=[sb_])
                        cur, shift = sb_, 0
                    dg = tp_b.get(T)
                    win = cur.sub(shift, shift + 512)
                    um = u.sub(8, 520)
                    dve(lambda e, dg=dg, win=win, um=um, w=w: e.scalar_tensor_tensor(
                        dg.r, win.f, 1.0 / w, um.f, ALU.mult, ALU.subtract), reads=[cur, u], writes=[dg])
                    rcb = (rnd * 4 + g) * 16
                    e8 = tp_b.get(16)
                    dve(lambda e, e8=e8, win=win, rcb=rcb: e.tensor_tensor(
                        e8.sub(0, 8).f, win.sub(0, 8).f, rc[:, rcb:rcb + 8], ALU.mult),
                        reads=[cur, ("t", "rc")], writes=[e8])
                    dve(lambda e, e8=e8, win=win, rcb=rcb: e.tensor_tensor(
                        e8.sub(8, 16).f, win.sub(504, 512).f, rc[:, rcb + 8:rcb + 16], ALU.mult),
                        reads=[cur, ("t", "rc")], writes=[e8])
                    dve(lambda e, e8=e8, dg=dg, um=um: e.tensor_tensor(
                        dg.sub(0, 8).r, e8.sub(0, 8).f, um.sub(0, 8).f, ALU.subtract),
                        reads=[e8, u, dg], writes=[dg])
                    dve(lambda e, e8=e8, dg=dg, um=um: e.tensor_tensor(
                        dg.sub(504, 512).r, e8.sub(8, 16).f, um.sub(504, 512).f, ALU.subtract),
                        reads=[e8, u, dg], writes=[dg])
                    dgs.append(dg)
                wp = ws.get()
                for co in range(4):
                    ch = g * 4 + co
                    bk = pb()
                    mm_group(bk, 512, [(wp.sub(k * 512 + co * 128, k * 512 + co * 128 + 128), dgs[k]) for k in range(4)],
                             reads=[wp] + dgs)
                    act(B1(ch).r, ps[:, bk, :], AF.Identity, scale=cvc(PS_ + ch), reads=[PSK(bk), CVK], writes=[B1(ch)])

            if rnd == 0:
                tap(1, B1)
            for j in range(16):
                bga, bpa, bgb, bpb = pb(), pb(), pb(), pb()
                wga = ws.get()
                mm_group(bga, 512, [(wga.sub(k * 128, (k + 1) * 128), xmain(k)) for k in range(16)], reads=[wga, xt])
                wpu = ws.get()
                mm_group(bpa, 512, [(wpu.sub(k * 128, (k + 1) * 128), B1(k)) for k in range(16)],
                         reads=[wpu] + [B1(k) for k in range(16)])
                wgb = ws.get()
                mm_group(bgb, 512, [(wgb.sub(k * 128, (k + 1) * 128), xmain(k)) for k in range(16)], reads=[wgb, xt])
                wlu = ws.get()
                mm_group(bpb, 512, [(wlu.sub(k * 128, (k + 1) * 128), B2(k)) for k in range(16)],
                         reads=[wlu] + [B2(k) for k in range(16)])
                ta, tb = tp_s.get(T), tp_s.get(T)
                act(ta.f, ps[:, bga, :], AF.Tanh, scale=0.5, reads=[PSK(bga)], writes=[ta])
                act(tb.f, ps[:, bgb, :], AF.Tanh, scale=0.5, reads=[PSK(bgb)], writes=[tb])
                dve(lambda e, ta=ta, bpa=bpa: e.scalar_tensor_tensor(ta.f, ta.f, 1.0, ps[:, bpa, :], ALU.add, ALU.mult),
                    reads=[ta, PSK(bpa)], writes=[ta])
                dve(lambda e, tb=tb, bpb=bpb: e.scalar_tensor_tensor(tb.f, tb.f, 1.0, ps[:, bpb, :], ALU.add, ALU.mult),
                    reads=[tb, PSK(bpb)], writes=[tb])
                dve(lambda e, ta=ta, tb=tb, j=j: e.tensor_tensor(B3(j).r, ta.f, tb.f, ALU.add),
                    reads=[ta, tb], writes=[B3(j)])

            if rnd == 0:
                tap(2, B3)
            def ln_finish():
                mean, rstd = LNM, LNR
                act(mean.f, ps[:, 6, :], AF.Copy, scale=1.0 / D, reads=[PSK(6)], writes=[mean])
                msq = tp_s.get(T)
                dve(lambda e, mean=mean, msq=msq: e.tensor_tensor(msq.f, mean.f, mean.f, ALU.mult),
                    reads=[mean], writes=[msq])
                dve(lambda e, msq=msq: e.scalar_tensor_tensor(msq.f, ps[:, 7, :], 1.0 / D, msq.f, ALU.mult, ALU.subtract),
                    reads=[msq, PSK(7)], writes=[msq])
                dve(lambda e, msq=msq: e.tensor_scalar(msq.f, msq.f, EPS, None, ALU.add), reads=[msq], writes=[msq])
                act(msq.f, msq.f, AF.Sqrt, reads=[msq], writes=[msq])
                dve(lambda e, msq=msq, rstd=rstd: e.reciprocal(rstd.f, msq.f), reads=[msq], writes=[rstd])
                dve(lambda e, mean=mean, rstd=rstd: e.scalar_tensor_tensor(mean.f, mean.f, -1.0, rstd.f, ALU.mult, ALU.mult),
                    reads=[mean, rstd], writes=[mean])
                return rstd, mean

            def ln_stats(j, src):
                sq = tp_s.get(T)
                act(sq.f, src.f, AF.Square, reads=[src], writes=[sq])
                P.op("pe", lambda e, j=j, src=src: e.matmul(ps[:, 6, :], ones[:], src.f, start=(j == 0), stop=(j == 15)),
                     reads=keys_of(src, ("t", "ones")), writes=[PSK(6)])
                P.op("pe", lambda e, j=j, sq=sq: e.matmul(ps[:, 7, :], ones[:], sq.f, start=(j == 0), stop=(j == 15)),
                     reads=keys_of(sq, ("t", "ones")), writes=[PSK(7)])

            for j in range(16):
                wo = ws.get()
                bk = pb()
                mm_group(bk, 512, [(wo.sub(k * 128, (k + 1) * 128), B3(k)) for k in range(16)],
                         reads=[wo] + [B3(k) for k in range(16)])
                xb = tp_s.get(T)
                dve(lambda e, xb=xb, j=j: e.tensor_scalar(xb.f, xmain(j).f, ALPHA, cvc(BO_ + j), ALU.mult, ALU.add),
                    reads=[xmain(j), CVK], writes=[xb])
                dve(lambda e, xb=xb, j=j, bk=bk: e.scalar_tensor_tensor(B1(j).f, ps[:, bk, :], 0.5, xb.f, ALU.mult, ALU.add),
                    reads=[xb, PSK(bk)], writes=[B1(j)])
                ln_stats(j, B1(j))
            if rnd == 0:
                tap(3, B1)
            rstd, nmr = ln_finish()
            for j in range(16):
                t = tp_s.get(T)
                dve(lambda e, t=t, j=j: e.tensor_tensor(t.f, B1(j).f, rstd.f, ALU.mult), reads=[B1(j), rstd], writes=[t])
                dve(lambda e, t=t: e.tensor_tensor(t.f, t.f, nmr.f, ALU.add), reads=[t, nmr], writes=[t])
                act(B2(j).r, t.f, AF.Identity, bias=cvc(B1_ + j), scale=cvc(G1_ + j), reads=[t, CVK], writes=[B2(j)])
                act(B1(j).f, t.f, AF.Identity, bias=dvc(AB1_ + j), scale=dvc(AG1_ + j), reads=[t, DVK], writes=[B1(j)])

            if rnd == 0:
                tap(4, B2)
            for qg in range(4):
                for fi in range(16):
                    w1 = ws.get()
                    bk = pb()
                    mm_group(bk, 512, [(w1.sub(k * 128, (k + 1) * 128), B2(k)) for k in range(16)],
                             reads=[w1] + [B2(k) for k in range(16)])
                    t = tp_s.get(T)
                    act(t.f, ps[:, bk, :], AF.Relu, bias=cvc(BF1_ + qg * 16 + fi), reads=[PSK(bk), CVK], writes=[t])
                    dve(lambda e, t=t, fi=fi: e.tensor_tensor(B3(fi).r, t.f, t.f, ALU.mult), reads=[t], writes=[B3(fi)])
                for j in range(16):
                    w2 = ws.get()
                    bk = pb()
                    mm_group(bk, 512, [(w2.sub(k * 128, (k + 1) * 128), B3(k)) for k in range(16)],
                             reads=[w2] + [B3(k) for k in range(16)])
                    dve(lambda e, j=j, bk=bk: e.tensor_tensor(B1(j).f, ps[:, bk, :], B1(j).f, ALU.add),
                        reads=[PSK(bk), B1(j)], writes=[B1(j)])
                    if qg == 3:
                        ln_stats(j, B1(j))

            if rnd == 0:
                tap(5, B1)
            rstd, nmr = ln_finish()
            for j in range(16):
                t = tp_s.get(T)
                dve(lambda e, t=t, j=j: e.tensor_tensor(t.f, B1(j).f, rstd.f, ALU.mult), reads=[B1(j), rstd], writes=[t])
                dve(lambda e, t=t: e.tensor_tensor(t.f, t.f, nmr.f, ALU.add), reads=[t, nmr], writes=[t])
                o = tp_s.get(T)
                act(o.f, t.f, AF.Identity, bias=cvc(B2_ + j), scale=cvc(G2_ + j), reads=[t, CVK], writes=[o])
                last_out_tok = P.op("sp", lambda e, o=o, j=j, rnd=rnd: e.dma_start(out=outT[rnd][:, j * T:(j + 1) * T], in_=o.f),
                                    reads=o.keys(), writes=[("dr", "out", rnd, j)], sem=f"st{j}")
        for j in range(16):
            P.final_wait("sp", (f"st{j}", P.cnt[f"st{j}"]))
            if DEBUG:
                P.final_wait("sp", (f"dbg{j}", P.cnt[f"dbg{j}"]))

        sems = {}
        for k in P.cnt:
            sems[k] = es.enter_context(nc.semaphore(str(k)))
        block = es.enter_context(nc.Block())
        engmap = {"pe": "tensor", "act": "scalar", "dve": "vector", "pool": "gpsimd", "sp": "sync"}

        def make(engname):
            items = P.q[engname]

            def body(e):
                for it in items:
                    if it[0] == "wait":
                        e.wait_ge(sems[it[1]], it[2])
                    else:
                        ins = it[1](e)
                        ins.then_inc(sems[it[2]], it[3])
            return body
        for en, attr in engmap.items():
            getattr(block, attr)(make(en))
    return nc


_CACHE = {}


def _tile_cols(w, ncol=128):
    K, N = w.shape
    kc = K // 128
    a = w.reshape(kc, 128, N // ncol, ncol).transpose(2, 1, 0, 3)
    return np.ascontiguousarray(a).reshape(N // ncol, 128, kc * ncol)


def kernel(x, w_in, pool_w, pool_scale, conv_w, conv_b, lru_wa, lru_ba, lru_wx, lru_bx,
           lru_lambda, w_pool_up, w_lru_up, w_out, b_out, ln1_g, ln1_b,
           w_ff1, b_ff1, w_ff2, b_ff2, ln2_g, ln2_b):
    f = np.float32
    x = np.asarray(x, f)
    if "nc" not in _CACHE:
        _CACHE["nc"] = build_program()
    nc = _CACHE["nc"]

    def vec(v):
        return np.asarray(v, f).reshape(-1, 128).T

    w_in_t = _tile_cols(np.asarray(w_in[0], f))
    pool_w_t = np.stack([
        np.ascontiguousarray(np.asarray(pool_w[0, g], f).reshape(4, 128, 512).transpose(1, 0, 2)).reshape(128, 2048)
        for g in range(4)])
    lw = np.zeros((8, 128, 2, 2, 2, 256), f)
    for d in range(2):
        for gate, wsrc in ((0, lru_wa), (1, lru_wx)):
            for h in range(8):
                m = np.asarray(wsrc[0, d, h], f)
                lw[h, :, d, gate] = m.reshape(2, 128, 256).transpose(1, 0, 2)
    lru_w_t = lw.reshape(8, 128, 2048)
    wpu_t = _tile_cols(np.asarray(w_pool_up[0], f))
    wlu_t = _tile_cols(np.asarray(w_lru_up[0], f))
    wout_t = _tile_cols(np.asarray(w_out[0], f))
    w1_t = _tile_cols(np.asarray(w_ff1[0], f))
    w2 = np.asarray(w_ff2[0], f)
    w2_t = np.concatenate([_tile_cols(w2[qg * 2048:(qg + 1) * 2048]) for qg in range(4)], axis=0)

    cvec = np.zeros((128, NV), f)
    cvec[:, PS_:PS_ + 16] = vec(pool_scale[0])
    for k in range(4):
        cvec[:, CW_ + k * 16:CW_ + (k + 1) * 16] = vec(conv_w[0, k])
    cvec[:, CB_:CB_ + 16] = vec(conv_b[0])
    for d in range(2):
        cvec[:, BA_ + d * 16:BA_ + (d + 1) * 16] = vec(lru_ba[0, d])
        cvec[:, BX_ + d * 16:BX_ + (d + 1) * 16] = vec(lru_bx[0, d])
        cvec[:, LAM_ + d * 16:LAM_ + (d + 1) * 16] = vec(lru_lambda[0, d])
    cvec[:, BO_:BO_ + 16] = vec(b_out[0])
    cvec[:, G1_:G1_ + 16] = vec(ln1_g[0])
    cvec[:, B1_:B1_ + 16] = vec(ln1_b[0])
    cvec[:, BF1_:BF1_ + 64] = vec(b_ff1[0])
    cvec[:, BF2_:BF2_ + 16] = vec(b_ff2[0])
    cvec[:, G2_:G2_ + 16] = vec(ln2_g[0])
    cvec[:, B2_:B2_ + 16] = vec(ln2_b[0])

    shared = dict(w_in_t=w_in_t, pool_w_t=pool_w_t, lru_w_t=lru_w_t, wpu_t=wpu_t, wlu_t=wlu_t,
                  wout_t=wout_t, w1_t=w1_t, w2_t=w2_t, cvec=cvec)

    in_maps = []
    for c in range(NCORE):
        b, q = c // 4, c % 4
        xb = x[b]
        xpad = np.zeros((S + 16, D), f)
        xpad[8:8 + S] = xb
        xm = np.zeros((2, 128, 16 * TH), f)
        msk = np.zeros((128, 16), f)
        rc = np.zeros((128, 128), f)
        for r in range(2):
            gb = 2 * q + r
            t0 = gb * T
            seg = xpad[t0:t0 + TH]
            xm[r] = seg.T.reshape(16, 128, TH).transpose(1, 0, 2).reshape(128, 16 * TH)
            msk[:, r * 8 + gb] = 1.0
            for g in range(4):
                w = 2 << g
                for e in range(16):
                    t = t0 + (e if e < 8 else T - 16 + e)
                    lo, hi = max(t - w // 2, 0), min(t + w // 2, S)
                    rc[:, (r * 4 + g) * 16 + e] = 1.0 / float(hi - lo)
        items = [(blk, 0) for blk in range(0, 2 * q)] + [(blk, 1) for blk in range(7, 2 * q, -1)]
        assert len(items) == NIT
        xs = np.zeros((NIT, 128, 16 * TL), f)
        lru_wi = np.zeros((NIT * 8, 128, 1024), f)
        itm = np.zeros((128, NI), f)
        for it, (blk, dr) in enumerate(items):
            t0 = blk * T
            if dr == 0:
                seg = xpad[t0 + 6:t0 + 6 + TL]
            else:
                seg = xpad[t0 + 5:t0 + 5 + TL][::-1]
            xs[it] = seg.T.reshape(16, 128, TL).transpose(1, 0, 2).reshape(128, 16 * TL)
            for k in range(4):
                kk = k if dr == 0 else 3 - k
                itm[:, CWI_ + it * 64 + k * 16:CWI_ + it * 64 + (k + 1) * 16] = vec(conv_w[0, kk])
            itm[:, BAI_ + it * 16:BAI_ + (it + 1) * 16] = vec(lru_ba[0, dr])
            itm[:, BXI_ + it * 16:BXI_ + (it + 1) * 16] = vec(lru_bx[0, dr])
            itm[:, LAMI_ + it * 16:LAMI_ + (it + 1) * 16] = vec(lru_lambda[0, dr])
            itm[:, KEEP_ + it] = 0.0 if (it == 0 or it == 2 * q) else 1.0
            itm[:, SELFA_ + it] = 1.0 if (q > 0 and it == 2 * q - 1) else 0.0
            itm[:, SELBB_ + it] = 1.0 if (it == 5 and q <= 2) else 0.0
            for h in range(8):
                for gate, wsrc in ((0, lru_wa), (1, lru_wx)):
                    mm = np.asarray(wsrc[0, dr, h], f)
                    lru_wi[it * 8 + h, :, gate * 512:(gate + 1) * 512] = mm.reshape(2, 128, 256).transpose(1, 0, 2).reshape(128, 512)
        m = dict(shared)
        m.update(xm=xm, xs=xs, rc=rc, msk=msk, lru_wi=lru_wi, itm=itm)
        in_maps.append(m)

    res = run_bass_kernel_spmd(nc, in_maps, core_ids=list(range(NCORE)))
    if DEBUG:
        _CACHE["dbg"] = [np.asarray(res.results[c]["dbg"]) for c in range(NCORE)]
    out = np.zeros((2, S, D), f)
    for c in range(NCORE):
        b, q = c // 4, c % 4
        o = np.asarray(res.results[c]["outT"], f).reshape(2, 128, 16, T)
        for r in range(2):
            gb = 2 * q + r
            out[b, gb * T:(gb + 1) * T, :] = o[r].transpose(2, 1, 0).reshape(T, D)
    return out
```

```python
import numpy as np
import concourse.bass as bass
import concourse.mybir as mybir
from concourse.bass_utils import run_bass_kernel_spmd

F32 = mybir.dt.float32
F32R = mybir.dt.float32r
AF = mybir.ActivationFunctionType
ALU = mybir.AluOpType
AX = mybir.AxisListType

D = 2048
S = 4096
NCORE = 8
T = 512
NBLK = 8
NIT = 7
TH = 528
TL = 516
ALPHA = 2.0 ** 0.25
EPS = 1e-5
CELL = 32

PS_, CW_, CB_, BA_, BX_, LAM_, BO_, G1_, B1_, BF1_, BF2_, G2_, B2_ = 0, 16, 80, 96, 128, 160, 192, 208, 224, 240, 304, 320, 336
NV = 352
HBA_, HBX_, CP_, HC_, HC512_, AG1_, AB1_ = 0, 32, 64, 96, 128, 160, 176
ND = 192
CWI_, BAI_, BXI_, LAMI_, KEEP_, SELFA_, SELBB_ = 0, 448, 560, 672, 784, 791, 798
NI = 808
HBAI_, HBXI_, CPI_, HCI_ = 0, 112, 224, 336
NDI = 448

ARENA = 50176
DEBUG = False
NTMP_SLOT = 528


class Base:
    def __init__(self, nc, a_off, off, n):
        self.nc, self.a_off, self.off, self.n = nc, a_off, off, n
        self._f = None
        self._r = None

    @property
    def F(self):
        if self._f is None:
            self._f = self.nc.alloc_sbuf_tensor_at(f"F{self.off}_{self.n}", [128, self.n], F32,
                                                   offset=self.a_off + self.off * 4)
        return self._f

    @property
    def R(self):
        if self._r is None:
            self._r = self.nc.alloc_sbuf_tensor_at(f"R{self.off}_{self.n}", [128, self.n], F32R,
                                                   offset=self.a_off + self.off * 4)
        return self._r


class V:
    def __init__(self, base, rel, n):
        self.base, self.rel, self.n = base, rel, n
        self.off = base.off + rel

    @property
    def f(self):
        return self.base.F[:, self.rel:self.rel + self.n]

    @property
    def r(self):
        return self.base.R[:, self.rel:self.rel + self.n]

    @property
    def rev(self):
        return self.base.F[:, self.rel:self.rel + self.n][:, ::-1]

    def sub(self, a, b):
        assert 0 <= a < b <= self.n
        return V(self.base, self.rel + a, b - a)

    def keys(self):
        return [("sb", c) for c in range(self.off // CELL, (self.off + self.n - 1) // CELL + 1)]


class Prog:
    ENG = ["pe", "act", "dve", "pool", "sp"]

    def __init__(self):
        self.q = {e: [] for e in self.ENG}
        self.cnt = {}
        self.waited = {e: {} for e in self.ENG}
        self.lw = {}
        self.rd = {}

    def op(self, eng, fn, reads=(), writes=(), sem=None):
        deps = {}

        def add(tok):
            if tok is None:
                return
            k, v = tok
            if deps.get(k, 0) < v:
                deps[k] = v
        for key in reads:
            add(self.lw.get(key))
        for key in writes:
            add(self.lw.get(key))
            for tok in self.rd.get(key, ()):
                add(tok)
        for k, v in deps.items():
            if k == eng and eng == "pe":
                continue
            if self.waited[eng].get(k, 0) >= v:
                continue
            self.waited[eng][k] = v
            self.q[eng].append(("wait", k, v))
        if sem is None:
            semk, inc = eng, 1
        else:
            semk, inc = sem, 16
        val = self.cnt.get(semk, 0) + inc
        self.cnt[semk] = val
        self.q[eng].append(("op", fn, semk, inc))
        tok = (semk, val)
        for key in writes:
            self.lw[key] = tok
            self.rd[key] = []
        for key in reads:
            self.rd.setdefault(key, []).append(tok)
        return tok

    def final_wait(self, eng, tok):
        self.q[eng].append(("wait", tok[0], tok[1]))


def keys_of(*items):
    out = []
    for it in items:
        if isinstance(it, V):
            out.extend(it.keys())
        elif isinstance(it, list):
            out.extend(it)
        else:
            out.append(it)
    return out


def build_program():
    nc = bass.Bass("TRN2", target_bir_lowering=False)
    xm = nc.dram_tensor("xm", [2, 128, 16 * TH], F32R, kind="ExternalInput").ap()
    xs = nc.dram_tensor("xs", [NIT, 128, 16 * TL], F32R, kind="ExternalInput").ap()
    lru_wi = nc.dram_tensor("lru_wi", [NIT * 8, 128, 1024], F32R, kind="ExternalInput").ap()
    itm_d = nc.dram_tensor("itm", [128, NI], F32, kind="ExternalInput").ap()
    w_in_t = nc.dram_tensor("w_in_t", [80, 128, 2048], F32R, kind="ExternalInput").ap()
    pool_w_t = nc.dram_tensor("pool_w_t", [4, 128, 2048], F32R, kind="ExternalInput").ap()
    lru_w_t = nc.dram_tensor("lru_w_t", [8, 128, 2048], F32R, kind="ExternalInput").ap()
    wpu_t = nc.dram_tensor("wpu_t", [16, 128, 2048], F32R, kind="ExternalInput").ap()
    wlu_t = nc.dram_tensor("wlu_t", [16, 128, 2048], F32R, kind="ExternalInput").ap()
    wout_t = nc.dram_tensor("wout_t", [16, 128, 2048], F32R, kind="ExternalInput").ap()
    w1_t = nc.dram_tensor("w1_t", [64, 128, 2048], F32R, kind="ExternalInput").ap()
    w2_t = nc.dram_tensor("w2_t", [64, 128, 2048], F32R, kind="ExternalInput").ap()
    cvec_d = nc.dram_tensor("cvec", [128, NV], F32, kind="ExternalInput").ap()
    rc_d = nc.dram_tensor("rc", [128, 128], F32, kind="ExternalInput").ap()
    msk_d = nc.dram_tensor("msk", [128, 16], F32, kind="ExternalInput").ap()
    outT = nc.dram_tensor("outT", [2, 128, 16 * T], F32, kind="ExternalOutput").ap()
    dbg = nc.dram_tensor("dbg", [8, 128, 16 * T], F32, kind="ExternalOutput").ap() if DEBUG else None

    P = Prog()

    import contextlib
    with contextlib.ExitStack() as es:
        cv = es.enter_context(nc.sbuf_tensor("cv", [128, NV], F32))
        dv = es.enter_context(nc.sbuf_tensor("dv", [128, ND], F32))
        rc = es.enter_context(nc.sbuf_tensor("rcs", [128, 128], F32))
        msk = es.enter_context(nc.sbuf_tensor("msks", [128, 16], F32))
        ones = es.enter_context(nc.sbuf_tensor("ones", [128, 128], F32R))
        ones_f = es.enter_context(nc.sbuf_tensor("ones_f", [128, 128], F32))
        scs = es.enter_context(nc.sbuf_tensor("scs", [128, 64], F32))
        itm = es.enter_context(nc.sbuf_tensor("itms", [128, NI], F32))
        dvi = es.enter_context(nc.sbuf_tensor("dvi", [128, NDI], F32))
        EST = es.enter_context(nc.sbuf_tensor("EST", [128, NIT * 16], F32))
        INI = es.enter_context(nc.sbuf_tensor("INI", [128, NIT * 16], F32))
        CFO = es.enter_context(nc.sbuf_tensor("CFO", [128, 32], F32))
        CBO = es.enter_context(nc.sbuf_tensor("CBO", [128, 32], F32))
        sm = es.enter_context(nc.sbuf_tensor("sm", [128, 64], F32))
        smi = es.enter_context(nc.sbuf_tensor("smi", [128, 224], F32))
        ps = es.enter_context(nc.psum_tensor("ps", [128, 8, 512], F32))
        a_off = (nc.sbuf_base + 63) // 64 * 64
        assert a_off + ARENA * 4 <= nc.sbuf_top, (a_off, nc.sbuf_top)
        bases = {}

        def A(off, n):
            assert off + n <= ARENA
            if (off, n) not in bases:
                bases[(off, n)] = Base(nc, a_off, off, n)
            return V(bases[(off, n)], 0, n)


        bank_ctr = [0]
        sc_ctr = [0]

        def pb():
            b = bank_ctr[0] % 6
            bank_ctr[0] += 1
            return b
        aux_ctr = [0]

        def pb_aux():
            b = 6 + aux_ctr[0] % 2
            aux_ctr[0] += 1
            return b

        def PSK(b):
            return ("ps", b)

        def mm_group(bank, n, pairs, reads, f32r=True, n0=0):
            def fn(e):
                last = None
                L = len(pairs)
                for i, (l, r) in enumerate(pairs):
                    if f32r:
                        la, ra = l.r, r.r
                    else:
                        la = l.f if isinstance(l, V) else l
                        ra = r.f
                    last = e.matmul(ps[:, bank, n0:n0 + n], la, ra, start=(i == 0), stop=(i == L - 1))
                return last
            P.op("pe", fn, reads=keys_of(*reads), writes=[PSK(bank)])

        def act(out, in_, func, bias=None, scale=None, reads=(), writes=()):
            kw = {}
            if bias is not None:
                kw["bias"] = bias
            if scale is not None:
                kw["scale"] = scale
            P.op("act", lambda e: e.activation(out=out, in_=in_, func=func, **kw),
                 reads=keys_of(*reads), writes=keys_of(*writes))

        def dve(fn, reads=(), writes=()):
            P.op("dve", fn, reads=keys_of(*reads), writes=keys_of(*writes))

        class WStream:
            def __init__(self, base, nslot, name):
                self.base, self.nslot, self.name = base, nslot, name
                self.plan = []
                self.issued = 0
                self.next = 0

            def slot(self, i):
                return A(self.base + (i % self.nslot) * 2048, 2048)

            def issue_to(self, upto):
                while self.issued < min(upto, len(self.plan)):
                    i = self.issued
                    src = self.plan[i][1]
                    dst = self.slot(i)
                    P.op("pool", (lambda s, d: (lambda e: e.dma_start(out=d.r, in_=s)))(src, dst),
                         reads=[], writes=dst.keys(), sem=f"{self.name}{i % self.nslot}")
                    self.issued += 1

            def get(self, tag=None):
                i = self.next
                assert tag is None or self.plan[i][0] == tag, (i, tag, self.plan[i][0])
                self.next += 1
                self.issue_to(i + self.nslot)
                return self.slot(i)

        def tap(idx, fnreg):
            if not DEBUG:
                return
            for j in range(16):
                reg = fnreg(j)
                P.op("sp", lambda e, reg=reg, j=j: e.dma_start(out=dbg[idx][:, j * T:(j + 1) * T], in_=reg.f),
                     reads=reg.keys(), writes=[("dr", "dbg", idx, j)], sem=f"dbg{j}")

        P.op("sp", lambda e: e.dma_start(out=cv[:], in_=cvec_d), writes=[("t", "cv")], sem="ld0")
        P.op("sp", lambda e: e.dma_start(out=rc[:], in_=rc_d), writes=[("t", "rc")], sem="ld1")
        P.op("sp", lambda e: e.dma_start(out=msk[:], in_=msk_d), writes=[("t", "msk")], sem="ld2")
        dve(lambda e: e.memset(ones_f[:], 1.0), writes=[("t", "ones_f")])
        act(ones[:], ones_f[:], AF.Copy, reads=[("t", "ones_f")], writes=[("t", "ones")])
        P.op("sp", lambda e: e.dma_start(out=itm[:], in_=itm_d), writes=[("t", "itm")], sem="ld3")
        ITK, DIK = ("t", "itm"), ("t", "dvi")
        dve(lambda e: e.tensor_scalar(dvi[:, HBAI_:HBAI_ + 224], itm[:, BAI_:BAI_ + 224], 0.5, None, ALU.mult),
            reads=[ITK], writes=[DIK])
        act(smi[:, 0:112], itm[:, LAMI_:LAMI_ + 112], AF.Exp, scale=-1.0, reads=[ITK], writes=[("t", "smi")])
        act(smi[:, 112:224], smi[:, 0:112], AF.Ln, bias=1.0, reads=[("t", "smi")], writes=[("t", "smi2")])
        dve(lambda e: e.tensor_scalar(dvi[:, CPI_:CPI_ + 112], smi[:, 112:224], -8.0, None, ALU.mult),
            reads=[("t", "smi2")], writes=[DIK])
        dve(lambda e: e.tensor_scalar(dvi[:, HCI_:HCI_ + 112], smi[:, 112:224], -4.0, None, ALU.mult),
            reads=[("t", "smi2")], writes=[DIK])
        CVK, DVK = ("t", "cv"), ("t", "dv")
        dve(lambda e: e.tensor_scalar(dv[:, HBA_:HBA_ + 64], cv[:, BA_:BA_ + 64], 0.5, None, ALU.mult),
            reads=[CVK], writes=[DVK])
        act(sm[:, 0:32], cv[:, LAM_:LAM_ + 32], AF.Exp, scale=-1.0, reads=[CVK], writes=[("t", "sm")])
        act(sm[:, 32:64], sm[:, 0:32], AF.Ln, bias=1.0, reads=[("t", "sm")], writes=[("t", "sm2")])
        dve(lambda e: e.tensor_scalar(dv[:, CP_:CP_ + 32], sm[:, 32:64], -8.0, None, ALU.mult),
            reads=[("t", "sm2")], writes=[DVK])
        dve(lambda e: e.tensor_scalar(dv[:, HC_:HC_ + 32], sm[:, 32:64], -4.0, None, ALU.mult),
            reads=[("t", "sm2")], writes=[DVK])
        dve(lambda e: e.tensor_scalar(dv[:, HC512_:HC512_ + 32], sm[:, 32:64], -4.0 * T, None, ALU.mult),
            reads=[("t", "sm2")], writes=[DVK])
        dve(lambda e: e.tensor_scalar(dv[:, AG1_:AG1_ + 16], cv[:, G1_:G1_ + 16], ALPHA, None, ALU.mult),
            reads=[CVK], writes=[DVK])
        dve(lambda e: e.scalar_tensor_tensor(dv[:, AB1_:AB1_ + 16], cv[:, B1_:B1_ + 16], ALPHA,
                                             cv[:, BF2_:BF2_ + 16], ALU.mult, ALU.add),
            reads=[CVK], writes=[DVK])

        def cvc(col):
            return cv[:, col:col + 1]

        def dvc(col):
            return dv[:, col:col + 1]

        class TPool:
            def __init__(self, regions):
                self.slots = []
                for (off, n) in regions:
                    k = n // NTMP_SLOT
                    for i in range(k):
                        self.slots.append(off + i * NTMP_SLOT)
                self.i = 0

            def get(self, n=T):
                off = self.slots[self.i % len(self.slots)]
                self.i += 1
                return A(off, n)

        def lru_head(h, xv, wnext, tp, dirs, cw, cb, outbox=None, pre=None, dbg_rnd=None, tpa=None, tph=None):
            tpa = tpa or tp
            if pre is not None:
                pre()
            xcs = []
            wu = [None, None]
            for ci in range(2):
                wu[ci] = wnext()
                bk = pb()
                mm_group(bk, 512, [(wu[ci].sub(k * 128, (k + 1) * 128), xv(k, 0, 512)) for k in range(16)],
                         reads=[wu[ci]] + [xv(k, 0, 512) for k in range(16)])
                bt = pb_aux()
                mm_group(bt, 4, [(wu[ci].sub(k * 128, (k + 1) * 128), xv(k, 512, 516)) for k in range(16)],
                         reads=[wu[ci]] + [xv(k, 512, 516) for k in range(16)])
                u = tpa.get(TL)
                act(u.sub(0, 512).f, ps[:, bk, 0:512], AF.Copy, reads=[PSK(bk)], writes=[u.sub(0, 512)])
                act(u.sub(512, 516).f, ps[:, bt, 0:4], AF.Copy, reads=[PSK(bt)], writes=[u.sub(512, 516)])
                ch = h * 2 + ci
                xc = tpa.get(T)
                (w0, wk0), (b0, bk0) = cw(0, ch), cb(ch)
                dve(lambda e, u=u, xc=xc, w0=w0, b0=b0: e.tensor_scalar(
                    xc.f, u.sub(0, 512).f, w0, b0, ALU.mult, ALU.add), reads=[u, wk0, bk0], writes=[xc])
                for k in (1, 2):
                    wk_, wkk = cw(k, ch)
                    dve(lambda e, u=u, xc=xc, k=k, wk_=wk_: e.scalar_tensor_tensor(
                        xc.f, u.sub(k, k + 512).f, wk_, xc.f, ALU.mult, ALU.add), reads=[u, xc, wkk], writes=[xc])
                w3, wk3 = cw(3, ch)
                dve(lambda e, u=u, xc=xc, w3=w3: e.scalar_tensor_tensor(
                    xc.r, u.sub(3, 515).f, w3, xc.f, ALU.mult, ALU.add), reads=[u, xc, wk3], writes=[xc])
                xcs.append(xc)
                if DEBUG and dbg_rnd == 0:
                    P.op("sp", lambda e, xc=xc, ch=ch: e.dma_start(out=dbg[6][:, ch * T:(ch + 1) * T], in_=xc.f),
                         reads=xc.keys(), writes=[("dr", "dbg", 6, ch)], sem=f"dbg{ch}")
            yield
            wg = wnext()
            items = []
            for di, dd in enumerate(dirs):
                for co in range(2):
                    ch = h * 2 + co
                    ba_, bx_ = pb(), pb()
                    for gate, bnk in ((0, ba_), (1, bx_)):
                        base = dd["gbase"](gate)
                        mm_group(bnk, 512,
                                 [(wg.sub(base + k * 256 + co * 128, base + k * 256 + co * 128 + 128), xcs[k])
                                  for k in range(2)], reads=[wg, xcs[0], xcs[1]])
                    t1, t2, t3 = tp.get(T), tp.get(T), tp.get(T)
                    (hba, k1), (hbx, k2), (hc, k3), (cp, k4) = dd["hba"](ch), dd["hbx"](ch), dd["hc"](ch), dd["cp"](ch)
                    act(t1.f, ps[:, ba_, :], AF.Tanh, bias=hba, scale=0.5, reads=[PSK(ba_), k1], writes=[t1])
                    act(t2.f, ps[:, bx_, :], AF.Tanh, bias=hbx, scale=0.5, reads=[PSK(bx_), k2], writes=[t2])
                    act(t3.f, t1.f, AF.Exp, bias=hc, scale=hc, reads=[t1, k3], writes=[t3])
                    act(t1.f, t3.f, AF.Square, reads=[t3, t1], writes=[t1])
                    items.append((di, dd, co, ch, t1, t2, t3))
            yield
            res = {}
            for (di, dd, co, ch, t1, t2, t3) in items:
                act(t1.f, t1.f, AF.Sqrt, bias=1.0, scale=-1.0, reads=[t1], writes=[t1])
                dve(lambda e, t2=t2, xc=xcs[co]: e.scalar_tensor_tensor(
                    t2.f, t2.f, 1.0, xc.f, ALU.add, ALU.mult), reads=[t2, xcs[co]], writes=[t2])
                dve(lambda e, t2=t2, t1=t1: e.scalar_tensor_tensor(
                    t2.f, t2.f, 0.5, t1.f, ALU.mult, ALU.mult), reads=[t2, t1], writes=[t2])
                init, ikeys = dd["init"](ch)
                if not dd["rev"]:
                    dve(lambda e, t1=t1, t3=t3, t2=t2, init=init: e.tensor_tensor_scan(
                        t1.f, t3.f, t2.f, init, ALU.mult, ALU.add), reads=[t3, t2] + ikeys, writes=[t1])
                else:
                    dve(lambda e, t1=t1, t3=t3, t2=t2, init=init: e.tensor_tensor_scan(
                        t1.rev, t3.rev, t2.rev, init, ALU.mult, ALU.add), reads=[t3, t2] + ikeys, writes=[t1])
                if dd.get("on_end") is not None:
                    dd["on_end"](ch, t1)
                res[(di, co)] = t1
            if outbox is not None:
                outs = []
                for co in range(2):
                    hf, hb = res[(0, co)], res[(1, co)]
                    ho = tph.get(T) if tph is not None else hf
                    dve(lambda e, hf=hf, hb=hb, ho=ho: e.tensor_tensor(ho.f, hf.f, hb.f, ALU.add),
                        reads=[hf, hb], writes=[ho])
                    hf = ho
                    outs.append(hf)
                    if DEBUG and dbg_rnd == 0:
                        chh = h * 2 + co
                        P.op("sp", lambda e, hf=hf, chh=chh: e.dma_start(out=dbg[7][:, chh * T:(chh + 1) * T], in_=hf.f),
                             reads=hf.keys(), writes=[("dr", "dbg", 7, chh)], sem=f"dbg{chh}")
                outbox.extend(outs)

        def run_pipe2(gens, after_c):
            n = len(gens)

            def fin(g):
                for _ in g:
                    pass
            next(gens[0])
            if n > 1:
                next(gens[1])
            next(gens[0])
            for i in range(1, n):
                if i + 1 < n:
                    next(gens[i + 1])
                fin(gens[i - 1])
                after_c(i - 1)
                next(gens[i])
            fin(gens[n - 1])
            after_c(n - 1)

        def run_pipe3(gens, after_c):
            n = len(gens)

            def fin(g):
                for _ in g:
                    pass
            next(gens[0])
            next(gens[1])
            next(gens[0])
            for i in range(1, n):
                if i + 1 < n:
                    next(gens[i + 1])
                fin(gens[i - 1])
                next(gens[i])
                after_c(i - 1)
            fin(gens[n - 1])
            after_c(n - 1)

        def run_pipe(gens, after_c):
            n = len(gens)

            def fin(g):
                for _ in g:
                    pass
            next(gens[0])
            next(gens[0])
            for i in range(1, n):
                next(gens[i])
                fin(gens[i - 1])
                after_c(i - 1)
                next(gens[i])
            fin(gens[n - 1])
            after_c(n - 1)

        HPG = 2
        L1_WL = 0
        L1_WG = HPG * 2 * 2048
        NWGI = 4
        L1_XS = L1_WG + NWGI * 1024
        L1_TMP = L1_XS + 2 * 16 * TL
        tp1 = TPool([(L1_TMP, ARENA - L1_TMP)])
        wgi_ctr = [0]

        def cw_item(it):
            return lambda k, ch: (itm[:, CWI_ + it * 64 + k * 16 + ch:CWI_ + it * 64 + k * 16 + ch + 1], ITK)

        def cb_main(ch):
            return (cvc(CB_ + ch), CVK)
        for hg in range(8 // HPG):
            wl = [A(L1_WL + i * 2048, 2048) for i in range(2 * HPG)]
            for hh in range(HPG):
                h = hg * HPG + hh
                for ci in range(2):
                    dst = wl[hh * 2 + ci]
                    P.op("pool", (lambda s_, d: (lambda e: e.dma_start(out=d.r, in_=s_)))(w_in_t[16 + h * 2 + ci], dst),
                         writes=dst.keys(), sem=f"wl{hh * 2 + ci}")
            gens = []
            for it in range(NIT):
                xb = A(L1_XS + (it % 2) * 16 * TL, 16 * TL)

                def load_xs(it=it, xb=xb):
                    P.op("pool", (lambda s_, d: (lambda e: e.dma_start(out=d.r, in_=s_)))(xs[it], xb),
                         writes=xb.keys(), sem=f"xs{it % 2}")

                def xv(k, a, b, xb=xb):
                    return xb.sub(k * TL + a, k * TL + b)
                for hh in range(HPG):
                    h = hg * HPG + hh
                    slot = wgi_ctr[0] % NWGI
                    wgi_ctr[0] += 1
                    wgi = A(L1_WG + slot * 1024, 1024)

                    def pre(it=it, h=h, wgi=wgi, slot=slot, first=(hh == 0), lx=load_xs):
                        if first:
                            lx()
                        P.op("pool", (lambda s_, d: (lambda e: e.dma_start(out=d.r, in_=s_)))(lru_wi[it * 8 + h], wgi),
                             writes=wgi.keys(), sem=f"wgi{slot}")

                    def mk_dir(it=it):
                        def col(base):
                            return lambda ch: (dvi[:, base + it * 16 + ch:base + it * 16 + ch + 1], DIK)

                        def init(ch):
                            if it == 0:
                                return 0.0, []
                            return INI[:, it * 16 + ch:it * 16 + ch + 1], [("t", "INI", it, ch)]

                        def on_end(ch, t1):
                            dve(lambda e, t1=t1: e.tensor_copy(EST[:, it * 16 + ch:it * 16 + ch + 1], t1.sub(511, 512).f),
                                reads=[t1], writes=[("t", "EST", it, ch)])
                            if it + 1 < NIT:
                                dve(lambda e: e.tensor_scalar(
                                    INI[:, (it + 1) * 16 + ch:(it + 1) * 16 + ch + 1], EST[:, it * 16 + ch:it * 16 + ch + 1],
                                    itm[:, KEEP_ + it + 1:KEEP_ + it + 2], None, ALU.mult),
                                    reads=[("t", "EST", it, ch), ITK], writes=[("t", "INI", it + 1, ch)])
                        return dict(gbase=lambda gate: gate * 512, hba=col(HBAI_), hbx=col(HBXI_), hc=col(HCI_),
                                    cp=col(CPI_), rev=False, init=init, on_end=on_end)
                    lst = [wl[hh * 2], wl[hh * 2 + 1], wgi]
                    gens.append(lru_head(h, xv, (lambda lst=lst: lst.pop(0)), tp1, [mk_dir()], cw_item(it), cb_main, pre=pre))
            run_pipe2(gens, lambda i: None)

        ESTALL = [("t", "EST", it, ch) for it in range(NIT) for ch in range(16)]
        dve(lambda e: e.memset(CFO[:], 0.0), reads=ESTALL, writes=[("t", "CO", 0, 0), ("t", "CO", 1, 0)])
        dve(lambda e: e.memset(CBO[:], 0.0), writes=[("t", "CO", 0, 1), ("t", "CO", 1, 1)])
        for it in range(NIT):
            dve(lambda e, it=it: e.scalar_tensor_tensor(
                CFO[:, 0:16], EST[:, it * 16:(it + 1) * 16], itm[:, SELFA_ + it:SELFA_ + it + 1],
                CFO[:, 0:16], ALU.mult, ALU.add), reads=[ITK, ("t", "CO", 0, 0)], writes=[("t", "CO", 0, 0)])
            dve(lambda e, it=it: e.scalar_tensor_tensor(
                CBO[:, 16:32], EST[:, it * 16:(it + 1) * 16], itm[:, SELBB_ + it:SELBB_ + it + 1],
                CBO[:, 16:32], ALU.mult, ALU.add), reads=[ITK, ("t", "CO", 1, 1)], writes=[("t", "CO", 1, 1)])
        dve(lambda e: e.tensor_copy(CBO[:, 0:16], EST[:, (NIT - 1) * 16:NIT * 16]),
            reads=[("t", "CO", 0, 1)], writes=[("t", "CO", 0, 1)])

        M_XT = 0
        M_B1 = 16 * TH
        M_B2 = M_B1 + 8192
        M_B3 = M_B2 + 8192
        M_WS = M_B3 + 8192
        NSLOT = 6
        M_TMP = M_WS + NSLOT * 2048
        LNR = A(M_TMP, T)
        LNM = A(M_TMP + NTMP_SLOT, T)
        M_TMP2 = M_TMP + 2 * NTMP_SLOT
        tp_s = TPool([(M_TMP2, ARENA - M_TMP2)])
        tp_b = TPool([(M_TMP2, ARENA - M_TMP2), (M_B3, 8192)])
        tp_m1a = TPool([(M_B1, 12 * NTMP_SLOT)])
        tp_hs = TPool([(M_TMP2, 4 * NTMP_SLOT)])
        tp_m1 = TPool([(M_TMP2 + 4 * NTMP_SLOT, ARENA - M_TMP2 - 4 * NTMP_SLOT), (M_B3, 8192),
                       (M_B1 + 12 * NTMP_SLOT, 8192 - 12 * NTMP_SLOT)])

        def B1(j):
            return A(M_B1 + j * 512, 512)

        def B2(j):
            return A(M_B2 + j * 512, 512)

        def B3(j):
            return A(M_B3 + j * 512, 512)

        last_out_tok = None
        for rnd in range(2):
            ws = WStream(M_WS, NSLOT, f"ws{rnd}_")
            plan = []

            def win(i):
                return (("win", i), w_in_t[i])
            plan += [win(16), win(17), win(18), win(19), (("lru", 0), lru_w_t[0])]
            for h in range(1, 8):
                if h + 1 < 8:
                    plan += [win(16 + 2 * (h + 1)), win(17 + 2 * (h + 1))]
                plan += [(("lru", h), lru_w_t[h]), win(32 + 2 * (h - 1)), win(33 + 2 * (h - 1))]
            plan += [win(32 + 14), win(33 + 14)]
            for g in range(4):
                plan += [win(g * 4 + ci) for ci in range(4)] + [(("pw", g), pool_w_t[g])]
            for j in range(16):
                plan += [win(48 + j), (("wpu", j), wpu_t[j]), win(64 + j), (("wlu", j), wlu_t[j])]
            for j in range(16):
                plan += [(("wout", j), wout_t[j])]
            for qg in range(4):
                plan += [(("w1", qg * 16 + fi), w1_t[qg * 16 + fi]) for fi in range(16)]
                plan += [(("w2", qg * 16 + j), w2_t[qg * 16 + j]) for j in range(16)]
            ws.plan = plan

            xt = A(M_XT, 16 * TH)
            P.op("pool", (lambda s, d: (lambda e: e.dma_start(out=d.r, in_=s)))(xm[rnd], xt),
                 writes=xt.keys(), sem="xt")

            def xmain(k, xt=xt):
                return xt.sub(k * TH + 8, k * TH + 520)

            def xv(k, a, b, xt=xt):
                return xt.sub(k * TH + 6 + a, k * TH + 6 + b)

            gens, boxes = [], []
            for h in range(8):
                tags = [("win", 16 + 2 * h), ("win", 17 + 2 * h), ("lru", h)]
                box = []
                boxes.append(box)
                def mk_dirs(rnd=rnd):
                    ds = []
                    for d in range(2):
                        def col(base, d=d):
                            return lambda ch: (dv[:, base + d * 16 + ch:base + d * 16 + ch + 1], DVK)

                        def init(ch, d=d):
                            src = CFO if d == 0 else CBO
                            return src[:, rnd * 16 + ch:rnd * 16 + ch + 1], [("t", "CO", rnd, d)]
                        on_end = None
                        if d == 0 and rnd == 0:
                            def on_end(ch, t1):
                                dve(lambda e, t1=t1: e.tensor_copy(CFO[:, 16 + ch:16 + ch + 1], t1.sub(511, 512).f),
                                    reads=[t1], writes=[("t", "CO", 1, 0)])
                        ds.append(dict(gbase=(lambda gate, d=d: ((d * 2 + gate) * 2) * 256), hba=col(HBA_), hbx=col(HBX_),
                                       hc=col(HC_), cp=col(CP_), rev=(d == 1), init=init, on_end=on_end))
                    return ds
                gens.append(lru_head(h, xv, (lambda tags=tags: ws.get(tags.pop(0))), tp_m1, mk_dirs(),
                                     (lambda k, ch: (cvc(CW_ + k * 16 + ch), CVK)), cb_main, outbox=box, dbg_rnd=rnd, tpa=tp_m1a, tph=tp_hs))

            def gelu_part(h):
                hs = boxes[h]
                st = []
                for ci in range(2):
                    bk = pb()
                    wgt = ws.get(("win", 32 + 2 * h + ci))
                    mm_group(bk, 512, [(wgt.sub(k * 128, (k + 1) * 128), xmain(k)) for k in range(16)],
                             reads=[wgt] + [xmain(k) for k in range(16)])
                    g1 = tp_m1.get(T)
                    act(g1.f, ps[:, bk, :], AF.Square, reads=[PSK(bk)], writes=[g1])
                    st.append((bk, g1))
                for (bk, g1) in st:
                    dve(lambda e, g1=g1, bk=bk: e.scalar_tensor_tensor(
                        g1.f, g1.f, 1.0 / 0.044715, ps[:, bk, :], ALU.add, ALU.mult),
                        reads=[g1, PSK(bk)], writes=[g1])
                for (bk, g1) in st:
                    act(g1.f, g1.f, AF.Tanh, scale=0.7978845608028654 * 0.044715, reads=[g1], writes=[g1])
                for ci, (bk, g1) in enumerate(st):
                    ch = h * 2 + ci
                    dve(lambda e, g1=g1, bk=bk: e.scalar_tensor_tensor(
                        g1.f, g1.f, 1.0, ps[:, bk, :], ALU.add, ALU.mult), reads=[g1, PSK(bk)], writes=[g1])
                    dve(lambda e, g1=g1, hsv=hs[ci], ch=ch: e.scalar_tensor_tensor(
                        B2(ch).r, hsv.f, 0.5, g1.f, ALU.mult, ALU.mult), reads=[g1, hs[ci]], writes=[B2(ch)])
            run_pipe3(gens, gelu_part)

            if rnd == 0:
                tap(0, B2)
            for g in range(4):
                w = 2 << g
                dgs = []
                for ci in range(4):
                    wsl = ws.get()
                    bk = pb()
                    mm_group(bk, 264, [(wsl.sub(k * 128, (k + 1) * 128), xt.sub(k * TH, k * TH + 264)) for k in range(16)],
                             reads=[wsl, xt])
                    bt = pb_aux()
                    mm_group(bt, 264, [(wsl.sub(k * 128, (k + 1) * 128), xt.sub(k * TH + 264, k * TH + 528)) for k in range(16)],
                             reads=[wsl, xt])
                    u = tp_b.get(TH)
                    act(u.sub(0, 264).f, ps[:, bk, 0:264], AF.Copy, reads=[PSK(bk)], writes=[u.sub(0, 264)])
                    act(u.sub(264, 528).f, ps[:, bt, 0:264], AF.Copy, reads=[PSK(bt)], writes=[u.sub(264, 528)])
                    sa, sb_ = tp_b.get(TH), tp_b.get(TH)
                    dve(lambda e, u=u, sa=sa: e.tensor_tensor(sa.sub(0, 527).f, u.sub(0, 527).f, u.sub(1, 528).f, ALU.add),
                        reads=[u], writes=[sa])
                    cur, shift = sa, 7
                    if g >= 1:
                        dve(lambda e, sa=sa, sb_=sb_: e.tensor_tensor(sb_.sub(0, 525).f, sa.sub(0, 525).f, sa.sub(2, 527).f, ALU.add),
                            reads=[sa], writes=[sb_])
                        cur, shift = sb_, 6
                    if g >= 2:
                        dve(lambda e, sa=sa, sb_=sb_: e.tensor_tensor(sa.sub(0, 521).f, sb_.sub(0, 521).f, sb_.sub(4, 525).f, ALU.add),
                            reads=[sb_], writes=[sa])
                        cur, shift = sa, 4
                    if g >= 3:
                        dve(lambda e, sa=sa, sb_=sb_: e.tensor_tensor(sb_.sub(0, 513).f, sa.sub(0, 513).f, sa.sub(8, 521).f, ALU.add),
                            reads=[sa], writes=[sb_])
                        cur, shift = sb_, 0
                    dg = tp_b.get(T)
                    win = cur.sub(shift, shift + 512)
                    um = u.sub(8, 520)
                    dve(lambda e, dg=dg, win=win, um=um, w=w: e.scalar_tensor_tensor(
                        dg.r, win.f, 1.0 / w, um.f, ALU.mult, ALU.subtract), reads=[cur, u], writes=[dg])
                    rcb = (rnd * 4 + g) * 16
                    e8 = tp_b.get(16)
                    dve(lambda e, e8=e8, win=win, rcb=rcb: e.tensor_tensor(
                        e8.sub(0, 8).f, win.sub(0, 8).f, rc[:, rcb:rcb + 8], ALU.mult),
                        reads=[cur, ("t", "rc")], writes=[e8])
                    dve(lambda e, e8=e8, win=win, rcb=rcb: e.tensor_tensor(
                        e8.sub(8, 16).f, win.sub(504, 512).f, rc[:, rcb + 8:rcb + 16], ALU.mult),
                        reads=[cur, ("t", "rc")], writes=[e8])
                    dve(lambda e, e8=e8, dg=dg, um=um: e.tensor_tensor(
                        dg.sub(0, 8).r, e8.sub(0, 8).f, um.sub(0, 8).f, ALU.subtract),
                        reads=[e8, u, dg], writes=[dg])
                    dve(lambda e, e8=e8, dg=dg, um=um: e.tensor_tensor(
                        dg.sub(504, 512).r, e8.sub(8, 16).f, um.sub(504, 512).f, ALU.subtract),
                        reads=[e8, u, dg], writes=[dg])
                    dgs.append(dg)
                wp = ws.get()
                for co in range(4):
                    ch = g * 4 + co
                    bk = pb()
                    mm_group(bk, 512, [(wp.sub(k * 512 + co * 128, k * 512 + co * 128 + 128), dgs[k]) for k in range(4)],
                             reads=[wp] + dgs)
                    act(B1(ch).r, ps[:, bk, :], AF.Identity, scale=cvc(PS_ + ch), reads=[PSK(bk), CVK], writes=[B1(ch)])

            if rnd == 0:
                tap(1, B1)
            for j in range(16):
                bga, bpa, bgb, bpb = pb(), pb(), pb(), pb()
                wga = ws.get()
                mm_group(bga, 512, [(wga.sub(k * 128, (k + 1) * 128), xmain(k)) for k in range(16)], reads=[wga, xt])
                wpu = ws.get()
                mm_group(bpa, 512, [(wpu.sub(k * 128, (k + 1) * 128), B1(k)) for k in range(16)],
                         reads=[wpu] + [B1(k) for k in range(16)])
                wgb = ws.get()
                mm_group(bgb, 512, [(wgb.sub(k * 128, (k + 1) * 128), xmain(k)) for k in range(16)], reads=[wgb, xt])
                wlu = ws.get()
                mm_group(bpb, 512, [(wlu.sub(k * 128, (k + 1) * 128), B2(k)) for k in range(16)],
                         reads=[wlu] + [B2(k) for k in range(16)])
                ta, tb = tp_s.get(T), tp_s.get(T)
                act(ta.f, ps[:, bga, :], AF.Tanh, scale=0.5, reads=[PSK(bga)], writes=[ta])
                act(tb.f, ps[:, bgb, :], AF.Tanh, scale=0.5, reads=[PSK(bgb)], writes=[tb])
                dve(lambda e, ta=ta, bpa=bpa: e.scalar_tensor_tensor(ta.f, ta.f, 1.0, ps[:, bpa, :], ALU.add, ALU.mult),
                    reads=[ta, PSK(bpa)], writes=[ta])
                dve(lambda e, tb=tb, bpb=bpb: e.scalar_tensor_tensor(tb.f, tb.f, 1.0, ps[:, bpb, :], ALU.add, ALU.mult),
                    reads=[tb, PSK(bpb)], writes=[tb])
                dve(lambda e, ta=ta, tb=tb, j=j: e.tensor_tensor(B3(j).r, ta.f, tb.f, ALU.add),
                    reads=[ta, tb], writes=[B3(j)])

            if rnd == 0:
                tap(2, B3)
            def ln_finish():
                mean, rstd = LNM, LNR
                act(mean.f, ps[:, 6, :], AF.Copy, scale=1.0 / D, reads=[PSK(6)], writes=[mean])
                msq = tp_s.get(T)
                dve(lambda e, mean=mean, msq=msq: e.tensor_tensor(msq.f, mean.f, mean.f, ALU.mult),
                    reads=[mean], writes=[msq])
                dve(lambda e, msq=msq: e.scalar_tensor_tensor(msq.f, ps[:, 7, :], 1.0 / D, msq.f, ALU.mult, ALU.subtract),
                    reads=[msq, PSK(7)], writes=[msq])
                dve(lambda e, msq=msq: e.tensor_scalar(msq.f, msq.f, EPS, None, ALU.add), reads=[msq], writes=[msq])
                act(msq.f, msq.f, AF.Sqrt, reads=[msq], writes=[msq])
                dve(lambda e, msq=msq, rstd=rstd: e.reciprocal(rstd.f, msq.f), reads=[msq], writes=[rstd])
                dve(lambda e, mean=mean, rstd=rstd: e.scalar_tensor_tensor(mean.f, mean.f, -1.0, rstd.f, ALU.mult, ALU.mult),
                    reads=[mean, rstd], writes=[mean])
                return rstd, mean

            def ln_stats(j, src):
                sq, cp = tp_s.get(T), tp_s.get(T)
                act(sq.r, src.f, AF.Square, reads=[src], writes=[sq])
                act(cp.r, src.f, AF.Copy, reads=[src], writes=[cp])
                P.op("pe", lambda e, j=j, cp=cp: e.matmul(ps[:, 6, :], ones[:], cp.r, start=(j == 0), stop=(j == 15)),
                     reads=keys_of(cp, ("t", "ones")), writes=[PSK(6)])
                P.op("pe", lambda e, j=j, sq=sq: e.matmul(ps[:, 7, :], ones[:], sq.r, start=(j == 0), stop=(j == 15)),
                     reads=keys_of(sq, ("t", "ones")), writes=[PSK(7)])

            for j in range(16):
                wo = ws.get()
                bk = pb()
                mm_group(bk, 512, [(wo.sub(k * 128, (k + 1) * 128), B3(k)) for k in range(16)],
                         reads=[wo] + [B3(k) for k in range(16)])
                xb = tp_s.get(T)
                dve(lambda e, xb=xb, j=j: e.tensor_scalar(xb.f, xmain(j).f, ALPHA, cvc(BO_ + j), ALU.mult, ALU.add),
                    reads=[xmain(j), CVK], writes=[xb])
                dve(lambda e, xb=xb, j=j, bk=bk: e.scalar_tensor_tensor(B1(j).f, ps[:, bk, :], 0.5, xb.f, ALU.mult, ALU.add),
                    reads=[xb, PSK(bk)], writes=[B1(j)])
                ln_stats(j, B1(j))
            if rnd == 0:
                tap(3, B1)
            rstd, nmr = ln_finish()
            for j in range(16):
                t = tp_s.get(T)
                dve(lambda e, t=t, j=j: e.tensor_tensor(t.f, B1(j).f, rstd.f, ALU.mult), reads=[B1(j), rstd], writes=[t])
                dve(lambda e, t=t: e.tensor_tensor(t.f, t.f, nmr.f, ALU.add), reads=[t, nmr], writes=[t])
                act(B2(j).r, t.f, AF.Identity, bias=cvc(B1_ + j), scale=cvc(G1_ + j), reads=[t, CVK], writes=[B2(j)])
                act(B1(j).f, t.f, AF.Identity, bias=dvc(AB1_ + j), scale=dvc(AG1_ + j), reads=[t, DVK], writes=[B1(j)])

            if rnd == 0:
                tap(4, B2)
            for qg in range(4):
                for fi in range(16):
                    w1 = ws.get()
                    bk = pb()
                    mm_group(bk, 512, [(w1.sub(k * 128, (k + 1) * 128), B2(k)) for k in range(16)],
                             reads=[w1] + [B2(k) for k in range(16)])
                    t = tp_s.get(T)
                    act(t.f, ps[:, bk, :], AF.Relu, bias=cvc(BF1_ + qg * 16 + fi), reads=[PSK(bk), CVK], writes=[t])
                    dve(lambda e, t=t, fi=fi: e.tensor_tensor(B3(fi).r, t.f, t.f, ALU.mult), reads=[t], writes=[B3(fi)])
                for j in range(16):
                    w2 = ws.get()
                    bk = pb()
                    mm_group(bk, 512, [(w2.sub(k * 128, (k + 1) * 128), B3(k)) for k in range(16)],
                             reads=[w2] + [B3(k) for k in range(16)])
                    dve(lambda e, j=j, bk=bk: e.tensor_tensor(B1(j).f, ps[:, bk, :], B1(j).f, ALU.add),
                        reads=[PSK(bk), B1(j)], writes=[B1(j)])
                    if qg == 3:
                        ln_stats(j, B1(j))

            if rnd == 0:
                tap(5, B1)
            rstd, nmr = ln_finish()
            for j in range(16):
                t = tp_s.get(T)
                dve(lambda e, t=t, j=j: e.tensor_tensor(t.f, B1(j).f, rstd.f, ALU.mult), reads=[B1(j), rstd], writes=[t])
                dve(lambda e, t=t: e.tensor_tensor(t.f, t.f, nmr.f, ALU.add), reads=[t, nmr], writes=[t])
                o = tp_s.get(T)
                act(o.f, t.f, AF.Identity, bias=cvc(B2_ + j), scale=cvc(G2_ + j), reads=[t, CVK], writes=[o])
                last_out_tok = P.op("sp", lambda e, o=o, j=j, rnd=rnd: e.dma_start(out=outT[rnd][:, j * T:(j + 1) * T], in_=o.f),
                                    reads=o.keys(), writes=[("dr", "out", rnd, j)], sem=f"st{j}")
        for j in range(16):
            P.final_wait("sp", (f"st{j}", P.cnt[f"st{j}"]))
            if DEBUG:
                P.final_wait("sp", (f"dbg{j}", P.cnt[f"dbg{j}"]))

        sems = {}
        for k in P.cnt:
            sems[k] = es.enter_context(nc.semaphore(str(k)))
        block = es.enter_context(nc.Block())
        engmap = {"pe": "tensor", "act": "scalar", "dve": "vector", "pool": "gpsimd", "sp": "sync"}

        def make(engname):
            items = P.q[engname]

            def body(e):
                for it in items:
                    if it[0] == "wait":
                        e.wait_ge(sems[it[1]], it[2])
                    else:
                        ins = it[1](e)
                        ins.then_inc(sems[it[2]], it[3])
            return body
        for en, attr in engmap.items():
            getattr(block, attr)(make(en))
    return nc


_CACHE = {}


def _tile_cols(w, ncol=128):
    K, N = w.shape
    kc = K // 128
    a = w.reshape(kc, 128, N // ncol, ncol).transpose(2, 1, 0, 3)
    return np.ascontiguousarray(a).reshape(N // ncol, 128, kc * ncol)


def kernel(x, w_in, pool_w, pool_scale, conv_w, conv_b, lru_wa, lru_ba, lru_wx, lru_bx,
           lru_lambda, w_pool_up, w_lru_up, w_out, b_out, ln1_g, ln1_b,
           w_ff1, b_ff1, w_ff2, b_ff2, ln2_g, ln2_b):
    f = np.float32
    x = np.asarray(x, f)
    if "nc" not in _CACHE:
        _CACHE["nc"] = build_program()
    nc = _CACHE["nc"]

    def vec(v):
        return np.asarray(v, f).reshape(-1, 128).T

    w_in_t = _tile_cols(np.asarray(w_in[0], f))
    pool_w_t = np.stack([
        np.ascontiguousarray(np.asarray(pool_w[0, g], f).reshape(4, 128, 512).transpose(1, 0, 2)).reshape(128, 2048)
        for g in range(4)])
    lw = np.zeros((8, 128, 2, 2, 2, 256), f)
    for d in range(2):
        for gate, wsrc in ((0, lru_wa), (1, lru_wx)):
            for h in range(8):
                m = np.asarray(wsrc[0, d, h], f)
                lw[h, :, d, gate] = m.reshape(2, 128, 256).transpose(1, 0, 2)
    lru_w_t = lw.reshape(8, 128, 2048)
    wpu_t = _tile_cols(np.asarray(w_pool_up[0], f))
    wlu_t = _tile_cols(np.asarray(w_lru_up[0], f))
    wout_t = _tile_cols(np.asarray(w_out[0], f))
    w1_t = _tile_cols(np.asarray(w_ff1[0], f))
    w2 = np.asarray(w_ff2[0], f)
    w2_t = np.concatenate([_tile_cols(w2[qg * 2048:(qg + 1) * 2048]) for qg in range(4)], axis=0)

    cvec = np.zeros((128, NV), f)
    cvec[:, PS_:PS_ + 16] = vec(pool_scale[0])
    for k in range(4):
        cvec[:, CW_ + k * 16:CW_ + (k + 1) * 16] = vec(conv_w[0, k])
    cvec[:, CB_:CB_ + 16] = vec(conv_b[0])
    for d in range(2):
        cvec[:, BA_ + d * 16:BA_ + (d + 1) * 16] = vec(lru_ba[0, d])
        cvec[:, BX_ + d * 16:BX_ + (d + 1) * 16] = vec(lru_bx[0, d])
        cvec[:, LAM_ + d * 16:LAM_ + (d + 1) * 16] = vec(lru_lambda[0, d])
    cvec[:, BO_:BO_ + 16] = vec(b_out[0])
    cvec[:, G1_:G1_ + 16] = vec(ln1_g[0])
    cvec[:, B1_:B1_ + 16] = vec(ln1_b[0])
    cvec[:, BF1_:BF1_ + 64] = vec(b_ff1[0])
    cvec[:, BF2_:BF2_ + 16] = vec(b_ff2[0])
    cvec[:, G2_:G2_ + 16] = vec(ln2_g[0])
    cvec[:, B2_:B2_ + 16] = vec(ln2_b[0])

    shared = dict(w_in_t=w_in_t, pool_w_t=pool_w_t, lru_w_t=lru_w_t, wpu_t=wpu_t, wlu_t=wlu_t,
                  wout_t=wout_t, w1_t=w1_t, w2_t=w2_t, cvec=cvec)

    in_maps = []
    for c in range(NCORE):
        b, q = c // 4, c % 4
        xb = x[b]
        xpad = np.zeros((S + 16, D), f)
        xpad[8:8 + S] = xb
        xm = np.zeros((2, 128, 16 * TH), f)
        msk = np.zeros((128, 16), f)
        rc = np.zeros((128, 128), f)
        for r in range(2):
            gb = 2 * q + r
            t0 = gb * T
            seg = xpad[t0:t0 + TH]
            xm[r] = seg.T.reshape(16, 128, TH).transpose(1, 0, 2).reshape(128, 16 * TH)
            msk[:, r * 8 + gb] = 1.0
            for g in range(4):
                w = 2 << g
                for e in range(16):
                    t = t0 + (e if e < 8 else T - 16 + e)
                    lo, hi = max(t - w // 2, 0), min(t + w // 2, S)
                    rc[:, (r * 4 + g) * 16 + e] = 1.0 / float(hi - lo)
        items = [(blk, 0) for blk in range(0, 2 * q)] + [(blk, 1) for blk in range(7, 2 * q, -1)]
        assert len(items) == NIT
        xs = np.zeros((NIT, 128, 16 * TL), f)
        lru_wi = np.zeros((NIT * 8, 128, 1024), f)
        itm = np.zeros((128, NI), f)
        for it, (blk, dr) in enumerate(items):
            t0 = blk * T
            if dr == 0:
                seg = xpad[t0 + 6:t0 + 6 + TL]
            else:
                seg = xpad[t0 + 5:t0 + 5 + TL][::-1]
            xs[it] = seg.T.reshape(16, 128, TL).transpose(1, 0, 2).reshape(128, 16 * TL)
            for k in range(4):
                kk = k if dr == 0 else 3 - k
                itm[:, CWI_ + it * 64 + k * 16:CWI_ + it * 64 + (k + 1) * 16] = vec(conv_w[0, kk])
            itm[:, BAI_ + it * 16:BAI_ + (it + 1) * 16] = vec(lru_ba[0, dr])
            itm[:, BXI_ + it * 16:BXI_ + (it + 1) * 16] = vec(lru_bx[0, dr])
            itm[:, LAMI_ + it * 16:LAMI_ + (it + 1) * 16] = vec(lru_lambda[0, dr])
            itm[:, KEEP_ + it] = 0.0 if (it == 0 or it == 2 * q) else 1.0
            itm[:, SELFA_ + it] = 1.0 if (q > 0 and it == 2 * q - 1) else 0.0
            itm[:, SELBB_ + it] = 1.0 if (it == 5 and q <= 2) else 0.0
            for h in range(8):
                for gate, wsrc in ((0, lru_wa), (1, lru_wx)):
                    mm = np.asarray(wsrc[0, dr, h], f)
                    lru_wi[it * 8 + h, :, gate * 512:(gate + 1) * 512] = mm.reshape(2, 128, 256).transpose(1, 0, 2).reshape(128, 512)
        m = dict(shared)
        m.update(xm=xm, xs=xs, rc=rc, msk=msk, lru_wi=lru_wi, itm=itm)
        in_maps.append(m)

    res = run_bass_kernel_spmd(nc, in_maps, core_ids=list(range(NCORE)))
    if DEBUG:
        _CACHE["dbg"] = [np.asarray(res.results[c]["dbg"]) for c in range(NCORE)]
    out = np.zeros((2, S, D), f)
    for c in range(NCORE):
        b, q = c // 4, c % 4
        o = np.asarray(res.results[c]["outT"], f).reshape(2, 128, 16, T)
        for r in range(2):
            gb = 2 * q + r
            out[b, gb * T:(gb + 1) * T, :] = o[r].transpose(2, 1, 0).reshape(T, D)
    return out
```

```python
import numpy as np
import concourse.bass as bass
import concourse.mybir as mybir
from concourse.bass_utils import run_bass_kernel_spmd

F32 = mybir.dt.float32
F32R = mybir.dt.float32r
AF = mybir.ActivationFunctionType
ALU = mybir.AluOpType
AX = mybir.AxisListType

D = 2048
S = 4096
NCORE = 8
T = 512
NBLK = 8
NIT = 7
TH = 528
TL = 516
ALPHA = 2.0 ** 0.25
EPS = 1e-5
CELL = 32

PS_, CW_, CB_, BA_, BX_, LAM_, BO_, G1_, B1_, BF1_, BF2_, G2_, B2_ = 0, 16, 80, 96, 128, 160, 192, 208, 224, 240, 304, 320, 336
NV = 352
HBA_, HBX_, CP_, HC_, HC512_, AG1_, AB1_ = 0, 32, 64, 96, 128, 160, 176
ND = 192
CWI_, BAI_, BXI_, LAMI_, KEEP_, SELFA_, SELBB_ = 0, 448, 560, 672, 784, 791, 798
NI = 808
HBAI_, HBXI_, CPI_, HCI_ = 0, 112, 224, 336
NDI = 448

ARENA = 50176
DEBUG = False
NTMP_SLOT = 528


class Base:
    def __init__(self, nc, a_off, off, n):
        self.nc, self.a_off, self.off, self.n = nc, a_off, off, n
        self._f = None
        self._r = None

    @property
    def F(self):
        if self._f is None:
            self._f = self.nc.alloc_sbuf_tensor_at(f"F{self.off}_{self.n}", [128, self.n], F32,
                                                   offset=self.a_off + self.off * 4)
        return self._f

    @property
    def R(self):
        if self._r is None:
            self._r = self.nc.alloc_sbuf_tensor_at(f"R{self.off}_{self.n}", [128, self.n], F32R,
                                                   offset=self.a_off + self.off * 4)
        return self._r


class V:
    def __init__(self, base, rel, n):
        self.base, self.rel, self.n = base, rel, n
        self.off = base.off + rel

    @property
    def f(self):
        return self.base.F[:, self.rel:self.rel + self.n]

    @property
    def r(self):
        return self.base.R[:, self.rel:self.rel + self.n]

    @property
    def rev(self):
        return self.base.F[:, self.rel:self.rel + self.n][:, ::-1]

    def sub(self, a, b):
        assert 0 <= a < b <= self.n
        return V(self.base, self.rel + a, b - a)

    def keys(self):
        return [("sb", c) for c in range(self.off // CELL, (self.off + self.n - 1) // CELL + 1)]


class Prog:
    ENG = ["pe", "act", "dve", "pool", "sp"]

    def __init__(self):
        self.q = {e: [] for e in self.ENG}
        self.cnt = {}
        self.waited = {e: {} for e in self.ENG}
        self.lw = {}
        self.rd = {}

    def op(self, eng, fn, reads=(), writes=(), sem=None):
        deps = {}

        def add(tok):
            if tok is None:
                return
            k, v = tok
            if deps.get(k, 0) < v:
                deps[k] = v
        for key in reads:
            add(self.lw.get(key))
        for key in writes:
            add(self.lw.get(key))
            for tok in self.rd.get(key, ()):
                add(tok)
        for k, v in deps.items():
            if k == eng and eng == "pe":
                continue
            if self.waited[eng].get(k, 0) >= v:
                continue
            self.waited[eng][k] = v
            self.q[eng].append(("wait", k, v))
        if sem is None:
            semk, inc = eng, 1
        else:
            semk, inc = sem, 16
        val = self.cnt.get(semk, 0) + inc
        self.cnt[semk] = val
        self.q[eng].append(("op", fn, semk, inc))
        tok = (semk, val)
        for key in writes:
            self.lw[key] = tok
            self.rd[key] = []
        for key in reads:
            self.rd.setdefault(key, []).append(tok)
        return tok

    def final_wait(self, eng, tok):
        self.q[eng].append(("wait", tok[0], tok[1]))


def keys_of(*items):
    out = []
    for it in items:
        if isinstance(it, V):
            out.extend(it.keys())
        elif isinstance(it, list):
            out.extend(it)
        else:
            out.append(it)
    return out


def build_program():
    nc = bass.Bass("TRN2", target_bir_lowering=False)
    xm = nc.dram_tensor("xm", [2, 128, 16 * TH], F32R, kind="ExternalInput").ap()
    xs = nc.dram_tensor("xs", [NIT, 128, 16 * TL], F32R, kind="ExternalInput").ap()
    lru_wi = nc.dram_tensor("lru_wi", [NIT * 8, 128, 1024], F32R, kind="ExternalInput").ap()
    itm_d = nc.dram_tensor("itm", [128, NI], F32, kind="ExternalInput").ap()
    w_in_t = nc.dram_tensor("w_in_t", [80, 128, 2048], F32R, kind="ExternalInput").ap()
    pool_w_t = nc.dram_tensor("pool_w_t", [4, 128, 2048], F32R, kind="ExternalInput").ap()
    lru_w_t = nc.dram_tensor("lru_w_t", [8, 128, 2048], F32R, kind="ExternalInput").ap()
    wpu_t = nc.dram_tensor("wpu_t", [16, 128, 2048], F32R, kind="ExternalInput").ap()
    wlu_t = nc.dram_tensor("wlu_t", [16, 128, 2048], F32R, kind="ExternalInput").ap()
    wout_t = nc.dram_tensor("wout_t", [16, 128, 2048], F32R, kind="ExternalInput").ap()
    w1_t = nc.dram_tensor("w1_t", [64, 128, 2048], F32R, kind="ExternalInput").ap()
    w2_t = nc.dram_tensor("w2_t", [64, 128, 2048], F32R, kind="ExternalInput").ap()
    cvec_d = nc.dram_tensor("cvec", [128, NV], F32, kind="ExternalInput").ap()
    rc_d = nc.dram_tensor("rc", [128, 128], F32, kind="ExternalInput").ap()
    msk_d = nc.dram_tensor("msk", [128, 16], F32, kind="ExternalInput").ap()
    outT = nc.dram_tensor("outT", [2, 128, 16 * T], F32, kind="ExternalOutput").ap()
    dbg = nc.dram_tensor("dbg", [8, 128, 16 * T], F32, kind="ExternalOutput").ap() if DEBUG else None

    P = Prog()

    import contextlib
    with contextlib.ExitStack() as es:
        cv = es.enter_context(nc.sbuf_tensor("cv", [128, NV], F32))
        dv = es.enter_context(nc.sbuf_tensor("dv", [128, ND], F32))
        rc = es.enter_context(nc.sbuf_tensor("rcs", [128, 128], F32))
        msk = es.enter_context(nc.sbuf_tensor("msks", [128, 16], F32))
        ones = es.enter_context(nc.sbuf_tensor("ones", [128, 128], F32R))
        ones_f = es.enter_context(nc.sbuf_tensor("ones_f", [128, 128], F32))
        scs = es.enter_context(nc.sbuf_tensor("scs", [128, 64], F32))
        itm = es.enter_context(nc.sbuf_tensor("itms", [128, NI], F32))
        dvi = es.enter_context(nc.sbuf_tensor("dvi", [128, NDI], F32))
        EST = es.enter_context(nc.sbuf_tensor("EST", [128, NIT * 16], F32))
        INI = es.enter_context(nc.sbuf_tensor("INI", [128, NIT * 16], F32))
        CFO = es.enter_context(nc.sbuf_tensor("CFO", [128, 32], F32))
        CBO = es.enter_context(nc.sbuf_tensor("CBO", [128, 32], F32))
        sm = es.enter_context(nc.sbuf_tensor("sm", [128, 64], F32))
        smi = es.enter_context(nc.sbuf_tensor("smi", [128, 224], F32))
        ps = es.enter_context(nc.psum_tensor("ps", [128, 8, 512], F32))
        a_off = (nc.sbuf_base + 63) // 64 * 64
        assert a_off + ARENA * 4 <= nc.sbuf_top, (a_off, nc.sbuf_top)
        bases = {}

        def A(off, n):
            assert off + n <= ARENA
            if (off, n) not in bases:
                bases[(off, n)] = Base(nc, a_off, off, n)
            return V(bases[(off, n)], 0, n)


        bank_ctr = [0]
        sc_ctr = [0]

        def pb():
            b = bank_ctr[0] % 6
            bank_ctr[0] += 1
            return b
        aux_ctr = [0]

        def pb_aux():
            b = 6 + aux_ctr[0] % 2
            aux_ctr[0] += 1
            return b

        def PSK(b):
            return ("ps", b)

        def mm_group(bank, n, pairs, reads, f32r=True, n0=0):
            def fn(e):
                last = None
                L = len(pairs)
                for i, (l, r) in enumerate(pairs):
                    if f32r:
                        la, ra = l.r, r.r
                    else:
                        la = l.f if isinstance(l, V) else l
                        ra = r.f
                    last = e.matmul(ps[:, bank, n0:n0 + n], la, ra, start=(i == 0), stop=(i == L - 1))
                return last
            P.op("pe", fn, reads=keys_of(*reads), writes=[PSK(bank)])

        def act(out, in_, func, bias=None, scale=None, reads=(), writes=()):
            kw = {}
            if bias is not None:
                kw["bias"] = bias
            if scale is not None:
                kw["scale"] = scale
            P.op("act", lambda e: e.activation(out=out, in_=in_, func=func, **kw),
                 reads=keys_of(*reads), writes=keys_of(*writes))

        def dve(fn, reads=(), writes=()):
            P.op("dve", fn, reads=keys_of(*reads), writes=keys_of(*writes))

        class WStream:
            def __init__(self, base, nslot, name):
                self.base, self.nslot, self.name = base, nslot, name
                self.plan = []
                self.issued = 0
                self.next = 0

            def slot(self, i):
                return A(self.base + (i % self.nslot) * 2048, 2048)

            def issue_to(self, upto):
                while self.issued < min(upto, len(self.plan)):
                    i = self.issued
                    src = self.plan[i][1]
                    dst = self.slot(i)
                    P.op("pool", (lambda s, d: (lambda e: e.dma_start(out=d.r, in_=s)))(src, dst),
                         reads=[], writes=dst.keys(), sem=f"{self.name}{i % self.nslot}")
                    self.issued += 1

            def get(self, tag=None):
                i = self.next
                assert tag is None or self.plan[i][0] == tag, (i, tag, self.plan[i][0])
                self.next += 1
                self.issue_to(i + self.nslot)
                return self.slot(i)

        def tap(idx, fnreg):
            if not DEBUG:
                return
            for j in range(16):
                reg = fnreg(j)
                P.op("sp", lambda e, reg=reg, j=j: e.dma_start(out=dbg[idx][:, j * T:(j + 1) * T], in_=reg.f),
                     reads=reg.keys(), writes=[("dr", "dbg", idx, j)], sem=f"dbg{j}")

        P.op("sp", lambda e: e.dma_start(out=cv[:], in_=cvec_d), writes=[("t", "cv")], sem="ld0")
        P.op("sp", lambda e: e.dma_start(out=rc[:], in_=rc_d), writes=[("t", "rc")], sem="ld1")
        P.op("sp", lambda e: e.dma_start(out=msk[:], in_=msk_d), writes=[("t", "msk")], sem="ld2")
        dve(lambda e: e.memset(ones_f[:], 1.0), writes=[("t", "ones_f")])
        act(ones[:], ones_f[:], AF.Copy, reads=[("t", "ones_f")], writes=[("t", "ones")])
        P.op("sp", lambda e: e.dma_start(out=itm[:], in_=itm_d), writes=[("t", "itm")], sem="ld3")
        ITK, DIK = ("t", "itm"), ("t", "dvi")
        dve(lambda e: e.tensor_scalar(dvi[:, HBAI_:HBAI_ + 224], itm[:, BAI_:BAI_ + 224], 0.5, None, ALU.mult),
            reads=[ITK], writes=[DIK])
        act(smi[:, 0:112], itm[:, LAMI_:LAMI_ + 112], AF.Exp, scale=-1.0, reads=[ITK], writes=[("t", "smi")])
        act(smi[:, 112:224], smi[:, 0:112], AF.Ln, bias=1.0, reads=[("t", "smi")], writes=[("t", "smi2")])
        dve(lambda e: e.tensor_scalar(dvi[:, CPI_:CPI_ + 112], smi[:, 112:224], -8.0, None, ALU.mult),
            reads=[("t", "smi2")], writes=[DIK])
        dve(lambda e: e.tensor_scalar(dvi[:, HCI_:HCI_ + 112], smi[:, 112:224], -4.0, None, ALU.mult),
            reads=[("t", "smi2")], writes=[DIK])
        CVK, DVK = ("t", "cv"), ("t", "dv")
        dve(lambda e: e.tensor_scalar(dv[:, HBA_:HBA_ + 64], cv[:, BA_:BA_ + 64], 0.5, None, ALU.mult),
            reads=[CVK], writes=[DVK])
        act(sm[:, 0:32], cv[:, LAM_:LAM_ + 32], AF.Exp, scale=-1.0, reads=[CVK], writes=[("t", "sm")])
        act(sm[:, 32:64], sm[:, 0:32], AF.Ln, bias=1.0, reads=[("t", "sm")], writes=[("t", "sm2")])
        dve(lambda e: e.tensor_scalar(dv[:, CP_:CP_ + 32], sm[:, 32:64], -8.0, None, ALU.mult),
            reads=[("t", "sm2")], writes=[DVK])
        dve(lambda e: e.tensor_scalar(dv[:, HC_:HC_ + 32], sm[:, 32:64], -4.0, None, ALU.mult),
            reads=[("t", "sm2")], writes=[DVK])
        dve(lambda e: e.tensor_scalar(dv[:, HC512_:HC512_ + 32], sm[:, 32:64], -4.0 * T, None, ALU.mult),
            reads=[("t", "sm2")], writes=[DVK])
        dve(lambda e: e.tensor_scalar(dv[:, AG1_:AG1_ + 16], cv[:, G1_:G1_ + 16], ALPHA, None, ALU.mult),
            reads=[CVK], writes=[DVK])
        dve(lambda e: e.scalar_tensor_tensor(dv[:, AB1_:AB1_ + 16], cv[:, B1_:B1_ + 16], ALPHA,
                                             cv[:, BF2_:BF2_ + 16], ALU.mult, ALU.add),
            reads=[CVK], writes=[DVK])

        def cvc(col):
            return cv[:, col:col + 1]

        def dvc(col):
            return dv[:, col:col + 1]

        class TPool:
            def __init__(self, regions):
                self.slots = []
                for (off, n) in regions:
                    k = n // NTMP_SLOT
                    for i in range(k):
                        self.slots.append(off + i * NTMP_SLOT)
                self.i = 0

            def get(self, n=T):
                off = self.slots[self.i % len(self.slots)]
                self.i += 1
                return A(off, n)

        def lru_head(h, xv, wnext, tp, dirs, cw, cb, outbox=None, pre=None, dbg_rnd=None, tpa=None, tph=None):
            tpa = tpa or tp
            if pre is not None:
                pre()
            xcs = []
            wu = [None, None]
            for ci in range(2):
                wu[ci] = wnext()
                bk = pb()
                mm_group(bk, 258, [(wu[ci].sub(k * 128, (k + 1) * 128), xv(k, 0, 258)) for k in range(16)],
                         reads=[wu[ci]] + [xv(k, 0, 258) for k in range(16)])
                bt = pb_aux()
                mm_group(bt, 258, [(wu[ci].sub(k * 128, (k + 1) * 128), xv(k, 258, 516)) for k in range(16)],
                         reads=[wu[ci]] + [xv(k, 258, 516) for k in range(16)])
                u = tpa.get(TL)
                act(u.sub(0, 258).f, ps[:, bk, 0:258], AF.Copy, reads=[PSK(bk)], writes=[u.sub(0, 258)])
                act(u.sub(258, 516).f, ps[:, bt, 0:258], AF.Copy, reads=[PSK(bt)], writes=[u.sub(258, 516)])
                ch = h * 2 + ci
                xc = tpa.get(T)
                (w0, wk0), (b0, bk0) = cw(0, ch), cb(ch)
                dve(lambda e, u=u, xc=xc, w0=w0, b0=b0: e.tensor_scalar(
                    xc.f, u.sub(0, 512).f, w0, b0, ALU.mult, ALU.add), reads=[u, wk0, bk0], writes=[xc])
                for k in (1, 2):
                    wk_, wkk = cw(k, ch)
                    dve(lambda e, u=u, xc=xc, k=k, wk_=wk_: e.scalar_tensor_tensor(
                        xc.f, u.sub(k, k + 512).f, wk_, xc.f, ALU.mult, ALU.add), reads=[u, xc, wkk], writes=[xc])
                w3, wk3 = cw(3, ch)
                dve(lambda e, u=u, xc=xc, w3=w3: e.scalar_tensor_tensor(
                    xc.r, u.sub(3, 515).f, w3, xc.f, ALU.mult, ALU.add), reads=[u, xc, wk3], writes=[xc])
                xcs.append(xc)
                if DEBUG and dbg_rnd == 0:
                    P.op("sp", lambda e, xc=xc, ch=ch: e.dma_start(out=dbg[6][:, ch * T:(ch + 1) * T], in_=xc.f),
                         reads=xc.keys(), writes=[("dr", "dbg", 6, ch)], sem=f"dbg{ch}")
            yield
            wg = wnext()
            items = []
            for di, dd in enumerate(dirs):
                for co in range(2):
                    ch = h * 2 + co
                    ba_, bx_ = pb(), pb()
                    for gate, bnk in ((0, ba_), (1, bx_)):
                        base = dd["gbase"](gate)
                        mm_group(bnk, 512,
                                 [(wg.sub(base + k * 256 + co * 128, base + k * 256 + co * 128 + 128), xcs[k])
                                  for k in range(2)], reads=[wg, xcs[0], xcs[1]])
                    t1, t2, t3 = tp.get(T), tp.get(T), tp.get(T)
                    (hba, k1), (hbx, k2), (hc, k3), (cp, k4) = dd["hba"](ch), dd["hbx"](ch), dd["hc"](ch), dd["cp"](ch)
                    act(t1.f, ps[:, ba_, :], AF.Tanh, bias=hba, scale=0.5, reads=[PSK(ba_), k1], writes=[t1])
                    act(t2.f, ps[:, bx_, :], AF.Tanh, bias=hbx, scale=0.5, reads=[PSK(bx_), k2], writes=[t2])
                    act(t3.f, t1.f, AF.Exp, bias=hc, scale=hc, reads=[t1, k3], writes=[t3])
                    act(t1.f, t3.f, AF.Square, reads=[t3, t1], writes=[t1])
                    items.append((di, dd, co, ch, t1, t2, t3))
            yield
            res = {}
            for (di, dd, co, ch, t1, t2, t3) in items:
                act(t1.f, t1.f, AF.Sqrt, bias=1.0, scale=-1.0, reads=[t1], writes=[t1])
                dve(lambda e, t2=t2, xc=xcs[co]: e.scalar_tensor_tensor(
                    t2.f, t2.f, 1.0, xc.f, ALU.add, ALU.mult), reads=[t2, xcs[co]], writes=[t2])
                dve(lambda e, t2=t2, t1=t1: e.scalar_tensor_tensor(
                    t2.f, t2.f, 0.5, t1.f, ALU.mult, ALU.mult), reads=[t2, t1], writes=[t2])
                init, ikeys = dd["init"](ch)
                if not dd["rev"]:
                    dve(lambda e, t1=t1, t3=t3, t2=t2, init=init: e.tensor_tensor_scan(
                        t1.f, t3.f, t2.f, init, ALU.mult, ALU.add), reads=[t3, t2] + ikeys, writes=[t1])
                else:
                    dve(lambda e, t1=t1, t3=t3, t2=t2, init=init: e.tensor_tensor_scan(
                        t1.rev, t3.rev, t2.rev, init, ALU.mult, ALU.add), reads=[t3, t2] + ikeys, writes=[t1])
                if dd.get("on_end") is not None:
                    dd["on_end"](ch, t1)
                res[(di, co)] = t1
            if outbox is not None:
                outs = []
                for co in range(2):
                    hf, hb = res[(0, co)], res[(1, co)]
                    ho = tph.get(T) if tph is not None else hf
                    dve(lambda e, hf=hf, hb=hb, ho=ho: e.tensor_tensor(ho.f, hf.f, hb.f, ALU.add),
                        reads=[hf, hb], writes=[ho])
                    hf = ho
                    outs.append(hf)
                    if DEBUG and dbg_rnd == 0:
                        chh = h * 2 + co
                        P.op("sp", lambda e, hf=hf, chh=chh: e.dma_start(out=dbg[7][:, chh * T:(chh + 1) * T], in_=hf.f),
                             reads=hf.keys(), writes=[("dr", "dbg", 7, chh)], sem=f"dbg{chh}")
                outbox.extend(outs)

        def run_pipe2(gens, after_c):
            n = len(gens)

            def fin(g):
                for _ in g:
                    pass
            next(gens[0])
            if n > 1:
                next(gens[1])
            next(gens[0])
            for i in range(1, n):
                if i + 1 < n:
                    next(gens[i + 1])
                fin(gens[i - 1])
                after_c(i - 1)
                next(gens[i])
            fin(gens[n - 1])
            after_c(n - 1)

        def run_pipe3(gens, after_c):
            n = len(gens)

            def fin(g):
                for _ in g:
                    pass
            next(gens[0])
            next(gens[1])
            next(gens[0])
            for i in range(1, n):
                if i + 1 < n:
                    next(gens[i + 1])
                fin(gens[i - 1])
                next(gens[i])
                after_c(i - 1)
            fin(gens[n - 1])
            after_c(n - 1)

        def run_pipe(gens, after_c):
            n = len(gens)

            def fin(g):
                for _ in g:
                    pass
            next(gens[0])
            next(gens[0])
            for i in range(1, n):
                next(gens[i])
                fin(gens[i - 1])
                after_c(i - 1)
                next(gens[i])
            fin(gens[n - 1])
            after_c(n - 1)

        HPG = 2
        L1_WL = 0
        L1_WG = HPG * 2 * 2048
        NWGI = 4
        L1_XS = L1_WG + NWGI * 1024
        L1_TMP = L1_XS + 2 * 16 * TL
        tp1 = TPool([(L1_TMP, ARENA - L1_TMP)])
        wgi_ctr = [0]

        def cw_item(it):
            return lambda k, ch: (itm[:, CWI_ + it * 64 + k * 16 + ch:CWI_ + it * 64 + k * 16 + ch + 1], ITK)

        def cb_main(ch):
            return (cvc(CB_ + ch), CVK)
        for hg in range(8 // HPG):
            wl = [A(L1_WL + i * 2048, 2048) for i in range(2 * HPG)]
            for hh in range(HPG):
                h = hg * HPG + hh
                for ci in range(2):
                    dst = wl[hh * 2 + ci]
                    P.op("pool", (lambda s_, d: (lambda e: e.dma_start(out=d.r, in_=s_)))(w_in_t[16 + h * 2 + ci], dst),
                         writes=dst.keys(), sem=f"wl{hh * 2 + ci}")
            gens = []
            for it in range(NIT):
                xb = A(L1_XS + (it % 2) * 16 * TL, 16 * TL)

                def load_xs(it=it, xb=xb):
                    P.op("pool", (lambda s_, d: (lambda e: e.dma_start(out=d.r, in_=s_)))(xs[it], xb),
                         writes=xb.keys(), sem=f"xs{it % 2}")

                def xv(k, a, b, xb=xb):
                    return xb.sub(k * TL + a, k * TL + b)
                for hh in range(HPG):
                    h = hg * HPG + hh
                    slot = wgi_ctr[0] % NWGI
                    wgi_ctr[0] += 1
                    wgi = A(L1_WG + slot * 1024, 1024)

                    def pre(it=it, h=h, wgi=wgi, slot=slot, first=(hh == 0), lx=load_xs):
                        if first:
                            lx()
                        P.op("pool", (lambda s_, d: (lambda e: e.dma_start(out=d.r, in_=s_)))(lru_wi[it * 8 + h], wgi),
                             writes=wgi.keys(), sem=f"wgi{slot}")

                    def mk_dir(it=it):
                        def col(base):
                            return lambda ch: (dvi[:, base + it * 16 + ch:base + it * 16 + ch + 1], DIK)

                        def init(ch):
                            if it == 0:
                                return 0.0, []
                            return INI[:, it * 16 + ch:it * 16 + ch + 1], [("t", "INI", it, ch)]

                        def on_end(ch, t1):
                            dve(lambda e, t1=t1: e.tensor_copy(EST[:, it * 16 + ch:it * 16 + ch + 1], t1.sub(511, 512).f),
                                reads=[t1], writes=[("t", "EST", it, ch)])
                            if it + 1 < NIT:
                                dve(lambda e, t1=t1: e.tensor_scalar(
                                    INI[:, (it + 1) * 16 + ch:(it + 1) * 16 + ch + 1], t1.sub(511, 512).f,
                                    itm[:, KEEP_ + it + 1:KEEP_ + it + 2], None, ALU.mult),
                                    reads=keys_of(t1, ITK), writes=[("t", "INI", it + 1, ch)])
                        return dict(gbase=lambda gate: gate * 512, hba=col(HBAI_), hbx=col(HBXI_), hc=col(HCI_),
                                    cp=col(CPI_), rev=False, init=init, on_end=on_end)
                    lst = [wl[hh * 2], wl[hh * 2 + 1], wgi]
                    gens.append(lru_head(h, xv, (lambda lst=lst: lst.pop(0)), tp1, [mk_dir()], cw_item(it), cb_main, pre=pre))
            run_pipe2(gens, lambda i: None)

        ESTALL = [("t", "EST", it, ch) for it in range(NIT) for ch in range(16)]
        dve(lambda e: e.memset(CFO[:], 0.0), reads=ESTALL, writes=[("t", "CO", 0, 0), ("t", "CO", 1, 0)])
        dve(lambda e: e.memset(CBO[:], 0.0), writes=[("t", "CO", 0, 1), ("t", "CO", 1, 1)])
        for it in range(NIT):
            dve(lambda e, it=it: e.scalar_tensor_tensor(
                CFO[:, 0:16], EST[:, it * 16:(it + 1) * 16], itm[:, SELFA_ + it:SELFA_ + it + 1],
                CFO[:, 0:16], ALU.mult, ALU.add), reads=[ITK, ("t", "CO", 0, 0)], writes=[("t", "CO", 0, 0)])
            dve(lambda e, it=it: e.scalar_tensor_tensor(
                CBO[:, 16:32], EST[:, it * 16:(it + 1) * 16], itm[:, SELBB_ + it:SELBB_ + it + 1],
                CBO[:, 16:32], ALU.mult, ALU.add), reads=[ITK, ("t", "CO", 1, 1)], writes=[("t", "CO", 1, 1)])
        dve(lambda e: e.tensor_copy(CBO[:, 0:16], EST[:, (NIT - 1) * 16:NIT * 16]),
            reads=[("t", "CO", 0, 1)], writes=[("t", "CO", 0, 1)])

        M_XT = 0
        M_B1 = 16 * TH
        M_B2 = M_B1 + 8192
        M_B3 = M_B2 + 8192
        M_WS = M_B3 + 8192
        NSLOT = 6
        M_TMP = M_WS + NSLOT * 2048
        LNR = A(M_TMP, T)
        LNM = A(M_TMP + NTMP_SLOT, T)
        M_TMP2 = M_TMP + 2 * NTMP_SLOT
        tp_s = TPool([(M_TMP2, ARENA - M_TMP2)])
        tp_b = TPool([(M_TMP2, ARENA - M_TMP2), (M_B3, 8192)])
        tp_m1a = TPool([(M_B1, 12 * NTMP_SLOT)])
        tp_hs = TPool([(M_TMP2, 4 * NTMP_SLOT)])
        tp_m1 = TPool([(M_TMP2 + 4 * NTMP_SLOT, ARENA - M_TMP2 - 4 * NTMP_SLOT), (M_B3, 8192),
                       (M_B1 + 12 * NTMP_SLOT, 8192 - 12 * NTMP_SLOT)])

        def B1(j):
            return A(M_B1 + j * 512, 512)

        def B2(j):
            return A(M_B2 + j * 512, 512)

        def B3(j):
            return A(M_B3 + j * 512, 512)

        last_out_tok = None
        for rnd in range(2):
            ws = WStream(M_WS, NSLOT, f"ws{rnd}_")
            plan = []

            def win(i):
                return (("win", i), w_in_t[i])
            plan += [win(16), win(17), win(18), win(19), (("lru", 0), lru_w_t[0])]
            for h in range(1, 8):
                if h + 1 < 8:
                    plan += [win(16 + 2 * (h + 1)), win(17 + 2 * (h + 1))]
                plan += [(("lru", h), lru_w_t[h]), win(32 + 2 * (h - 1)), win(33 + 2 * (h - 1))]
            plan += [win(32 + 14), win(33 + 14)]
            for g in range(4):
                plan += [win(g * 4 + ci) for ci in range(4)] + [(("pw", g), pool_w_t[g])]
            for j in range(16):
                plan += [win(48 + j), (("wpu", j), wpu_t[j]), win(64 + j), (("wlu", j), wlu_t[j])]
            for j in range(16):
                plan += [(("wout", j), wout_t[j])]
            for qg in range(4):
                plan += [(("w1", qg * 16 + fi), w1_t[qg * 16 + fi]) for fi in range(16)]
                plan += [(("w2", qg * 16 + j), w2_t[qg * 16 + j]) for j in range(16)]
            ws.plan = plan

            xt = A(M_XT, 16 * TH)
            P.op("pool", (lambda s, d: (lambda e: e.dma_start(out=d.r, in_=s)))(xm[rnd], xt),
                 writes=xt.keys(), sem="xt")

            def xmain(k, xt=xt):
                return xt.sub(k * TH + 8, k * TH + 520)

            def xv(k, a, b, xt=xt):
                return xt.sub(k * TH + 6 + a, k * TH + 6 + b)

            gens, boxes = [], []
            for h in range(8):
                tags = [("win", 16 + 2 * h), ("win", 17 + 2 * h), ("lru", h)]
                box = []
                boxes.append(box)
                def mk_dirs(rnd=rnd):
                    ds = []
                    for d in range(2):
                        def col(base, d=d):
                            return lambda ch: (dv[:, base + d * 16 + ch:base + d * 16 + ch + 1], DVK)

                        def init(ch, d=d):
                            src = CFO if d == 0 else CBO
                            return src[:, rnd * 16 + ch:rnd * 16 + ch + 1], [("t", "CO", rnd, d)]
                        on_end = None
                        if d == 0 and rnd == 0:
                            def on_end(ch, t1):
                                dve(lambda e, t1=t1: e.tensor_copy(CFO[:, 16 + ch:16 + ch + 1], t1.sub(511, 512).f),
                                    reads=[t1], writes=[("t", "CO", 1, 0)])
                        ds.append(dict(gbase=(lambda gate, d=d: ((d * 2 + gate) * 2) * 256), hba=col(HBA_), hbx=col(HBX_),
                                       hc=col(HC_), cp=col(CP_), rev=(d == 1), init=init, on_end=on_end))
                    return ds
                gens.append(lru_head(h, xv, (lambda tags=tags: ws.get(tags.pop(0))), tp_m1, mk_dirs(),
                                     (lambda k, ch: (cvc(CW_ + k * 16 + ch), CVK)), cb_main, outbox=box, dbg_rnd=rnd, tpa=tp_m1a, tph=tp_hs))

            def gelu_part(h):
                hs = boxes[h]
                st = []
                for ci in range(2):
                    bk = pb()
                    wgt = ws.get(("win", 32 + 2 * h + ci))
                    mm_group(bk, 512, [(wgt.sub(k * 128, (k + 1) * 128), xmain(k)) for k in range(16)],
                             reads=[wgt] + [xmain(k) for k in range(16)])
                    g1 = tp_m1.get(T)
                    act(g1.f, ps[:, bk, :], AF.Square, reads=[PSK(bk)], writes=[g1])
                    st.append((bk, g1))
                for (bk, g1) in st:
                    dve(lambda e, g1=g1, bk=bk: e.scalar_tensor_tensor(
                        g1.f, g1.f, 1.0 / 0.044715, ps[:, bk, :], ALU.add, ALU.mult),
                        reads=[g1, PSK(bk)], writes=[g1])
                for (bk, g1) in st:
                    act(g1.f, g1.f, AF.Tanh, scale=0.7978845608028654 * 0.044715, reads=[g1], writes=[g1])
                for ci, (bk, g1) in enumerate(st):
                    ch = h * 2 + ci
                    dve(lambda e, g1=g1, bk=bk: e.scalar_tensor_tensor(
                        g1.f, g1.f, 1.0, ps[:, bk, :], ALU.add, ALU.mult), reads=[g1, PSK(bk)], writes=[g1])
                    dve(lambda e, g1=g1, hsv=hs[ci], ch=ch: e.scalar_tensor_tensor(
                        B2(ch).r, hsv.f, 0.5, g1.f, ALU.mult, ALU.mult), reads=[g1, hs[ci]], writes=[B2(ch)])
            run_pipe3(gens, gelu_part)

            if rnd == 0:
                tap(0, B2)
            for g in range(4):
                w = 2 << g
                dgs = []
                for ci in range(4):
                    wsl = ws.get()
                    bk = pb()
                    mm_group(bk, 264, [(wsl.sub(k * 128, (k + 1) * 128), xt.sub(k * TH, k * TH + 264)) for k in range(16)],
                             reads=[wsl, xt])
                    bt = pb_aux()
                    mm_group(bt, 264, [(wsl.sub(k * 128, (k + 1) * 128), xt.sub(k * TH + 264, k * TH + 528)) for k in range(16)],
                             reads=[wsl, xt])
                    u = tp_b.get(TH)
                    act(u.sub(0, 264).f, ps[:, bk, 0:264], AF.Copy, reads=[PSK(bk)], writes=[u.sub(0, 264)])
                    act(u.sub(264, 528).f, ps[:, bt, 0:264], AF.Copy, reads=[PSK(bt)], writes=[u.sub(264, 528)])
                    sa, sb_ = tp_b.get(TH), tp_b.get(TH)
                    dve(lambda e, u=u, sa=sa: e.tensor_tensor(sa.sub(0, 527).f, u.sub(0, 527).f, u.sub(1, 528).f, ALU.add),
                        reads=[u], writes=[sa])
                    cur, shift = sa, 7
                    if g >= 1:
                        dve(lambda e, sa=sa, sb_=sb_: e.tensor_tensor(sb_.sub(0, 525).f, sa.sub(0, 525).f, sa.sub(2, 527).f, ALU.add),
                            reads=[sa], writes=[sb_])
                        cur, shift = sb_, 6
                    if g >= 2:
                        dve(lambda e, sa=sa, sb_=sb_: e.tensor_tensor(sa.sub(0, 521).f, sb_.sub(0, 521).f, sb_.sub(4, 525).f, ALU.add),
                            reads=[sb_], writes=[sa])
                        cur, shift = sa, 4
                    if g >= 3:
                        dve(lambda e, sa=sa, sb_=sb_: e.tensor_tensor(sb_.sub(0, 513).f, sa.sub(0, 513).f, sa.sub(8, 521).f, ALU.add),
                            reads=[sa], writes=[sb_])
                        cur, shift = sb_, 0
                    dg = tp_b.get(T)
                    win = cur.sub(shift, shift + 512)
                    um = u.sub(8, 520)
                    dve(lambda e, dg=dg, win=win, um=um, w=w: e.scalar_tensor_tensor(
                        dg.r, win.f, 1.0 / w, um.f, ALU.mult, ALU.subtract), reads=[cur, u], writes=[dg])
                    rcb = (rnd * 4 + g) * 16
                    e8 = tp_b.get(16)
                    dve(lambda e, e8=e8, win=win, rcb=rcb: e.tensor_tensor(
                        e8.sub(0, 8).f, win.sub(0, 8).f, rc[:, rcb:rcb + 8], ALU.mult),
                        reads=[cur, ("t", "rc")], writes=[e8])
                    dve(lambda e, e8=e8, win=win, rcb=rcb: e.tensor_tensor(
                        e8.sub(8, 16).f, win.sub(504, 512).f, rc[:, rcb + 8:rcb + 16], ALU.mult),
                        reads=[cur, ("t", "rc")], writes=[e8])
                    dve(lambda e, e8=e8, dg=dg, um=um: e.tensor_tensor(
                        dg.sub(0, 8).r, e8.sub(0, 8).f, um.sub(0, 8).f, ALU.subtract),
                        reads=[e8, u, dg], writes=[dg])
                    dve(lambda e, e8=e8, dg=dg, um=um: e.tensor_tensor(
                        dg.sub(504, 512).r, e8.sub(8, 16).f, um.sub(504, 512).f, ALU.subtract),
                        reads=[e8, u, dg], writes=[dg])
                    dgs.append(dg)
                wp = ws.get()
                for co in range(4):
                    ch = g * 4 + co
                    bk = pb()
                    mm_group(bk, 512, [(wp.sub(k * 512 + co * 128, k * 512 + co * 128 + 128), dgs[k]) for k in range(4)],
                             reads=[wp] + dgs)
                    act(B1(ch).r, ps[:, bk, :], AF.Identity, scale=cvc(PS_ + ch), reads=[PSK(bk), CVK], writes=[B1(ch)])

            if rnd == 0:
                tap(1, B1)
            for j in range(16):
                bga, bpa, bgb, bpb = pb(), pb(), pb(), pb()
                wga = ws.get()
                mm_group(bga, 512, [(wga.sub(k * 128, (k + 1) * 128), xmain(k)) for k in range(16)], reads=[wga, xt])
                wpu = ws.get()
                mm_group(bpa, 512, [(wpu.sub(k * 128, (k + 1) * 128), B1(k)) for k in range(16)],
                         reads=[wpu] + [B1(k) for k in range(16)])
                wgb = ws.get()
                mm_group(bgb, 512, [(wgb.sub(k * 128, (k + 1) * 128), xmain(k)) for k in range(16)], reads=[wgb, xt])
                wlu = ws.get()
                mm_group(bpb, 512, [(wlu.sub(k * 128, (k + 1) * 128), B2(k)) for k in range(16)],
                         reads=[wlu] + [B2(k) for k in range(16)])
                ta, tb = tp_s.get(T), tp_s.get(T)
                act(ta.f, ps[:, bga, :], AF.Tanh, scale=0.5, reads=[PSK(bga)], writes=[ta])
                act(tb.f, ps[:, bgb, :], AF.Tanh, scale=0.5, reads=[PSK(bgb)], writes=[tb])
                dve(lambda e, ta=ta, bpa=bpa: e.scalar_tensor_tensor(ta.f, ta.f, 1.0, ps[:, bpa, :], ALU.add, ALU.mult),
                    reads=[ta, PSK(bpa)], writes=[ta])
                dve(lambda e, tb=tb, bpb=bpb: e.scalar_tensor_tensor(tb.f, tb.f, 1.0, ps[:, bpb, :], ALU.add, ALU.mult),
                    reads=[tb, PSK(bpb)], writes=[tb])
                dve(lambda e, ta=ta, tb=tb, j=j: e.tensor_tensor(B3(j).r, ta.f, tb.f, ALU.add),
                    reads=[ta, tb], writes=[B3(j)])

            if rnd == 0:
                tap(2, B3)
            def ln_finish():
                mean, rstd = LNM, LNR
                act(mean.f, ps[:, 6, :], AF.Copy, scale=1.0 / D, reads=[PSK(6)], writes=[mean])
                msq = tp_s.get(T)
                dve(lambda e, mean=mean, msq=msq: e.tensor_tensor(msq.f, mean.f, mean.f, ALU.mult),
                    reads=[mean], writes=[msq])
                dve(lambda e, msq=msq: e.scalar_tensor_tensor(msq.f, ps[:, 7, :], 1.0 / D, msq.f, ALU.mult, ALU.subtract),
                    reads=[msq, PSK(7)], writes=[msq])
                dve(lambda e, msq=msq: e.tensor_scalar(msq.f, msq.f, EPS, None, ALU.add), reads=[msq], writes=[msq])
                act(msq.f, msq.f, AF.Sqrt, reads=[msq], writes=[msq])
                dve(lambda e, msq=msq, rstd=rstd: e.reciprocal(rstd.f, msq.f), reads=[msq], writes=[rstd])
                dve(lambda e, mean=mean, rstd=rstd: e.scalar_tensor_tensor(mean.f, mean.f, -1.0, rstd.f, ALU.mult, ALU.mult),
                    reads=[mean, rstd], writes=[mean])
                return rstd, mean

            def ln_stats(j, src):
                sq, cp = tp_s.get(T), tp_s.get(T)
                act(sq.r, src.f, AF.Square, reads=[src], writes=[sq])
                act(cp.r, src.f, AF.Copy, reads=[src], writes=[cp])
                P.op("pe", lambda e, j=j, cp=cp: e.matmul(ps[:, 6, :], ones[:], cp.r, start=(j == 0), stop=(j == 15)),
                     reads=keys_of(cp, ("t", "ones")), writes=[PSK(6)])
                P.op("pe", lambda e, j=j, sq=sq: e.matmul(ps[:, 7, :], ones[:], sq.r, start=(j == 0), stop=(j == 15)),
                     reads=keys_of(sq, ("t", "ones")), writes=[PSK(7)])

            for j in range(16):
                wo = ws.get()
                bk = pb()
                mm_group(bk, 512, [(wo.sub(k * 128, (k + 1) * 128), B3(k)) for k in range(16)],
                         reads=[wo] + [B3(k) for k in range(16)])
                xb = tp_s.get(T)
                dve(lambda e, xb=xb, j=j: e.tensor_scalar(xb.f, xmain(j).f, ALPHA, cvc(BO_ + j), ALU.mult, ALU.add),
                    reads=[xmain(j), CVK], writes=[xb])
                dve(lambda e, xb=xb, j=j, bk=bk: e.scalar_tensor_tensor(B1(j).f, ps[:, bk, :], 0.5, xb.f, ALU.mult, ALU.add),
                    reads=[xb, PSK(bk)], writes=[B1(j)])
                ln_stats(j, B1(j))
            if rnd == 0:
                tap(3, B1)
            rstd, nmr = ln_finish()
            for j in range(16):
                t = tp_s.get(T)
                dve(lambda e, t=t, j=j: e.tensor_tensor(t.f, B1(j).f, rstd.f, ALU.mult), reads=[B1(j), rstd], writes=[t])
                dve(lambda e, t=t: e.tensor_tensor(t.f, t.f, nmr.f, ALU.add), reads=[t, nmr], writes=[t])
                act(B2(j).r, t.f, AF.Identity, bias=cvc(B1_ + j), scale=cvc(G1_ + j), reads=[t, CVK], writes=[B2(j)])
                act(B1(j).f, t.f, AF.Identity, bias=dvc(AB1_ + j), scale=dvc(AG1_ + j), reads=[t, DVK], writes=[B1(j)])

            if rnd == 0:
                tap(4, B2)
            for qg in range(4):
                for fi in range(16):
                    w1 = ws.get()
                    bk = pb()
                    mm_group(bk, 512, [(w1.sub(k * 128, (k + 1) * 128), B2(k)) for k in range(16)],
                             reads=[w1] + [B2(k) for k in range(16)])
                    t = tp_s.get(T)
                    act(t.f, ps[:, bk, :], AF.Relu, bias=cvc(BF1_ + qg * 16 + fi), reads=[PSK(bk), CVK], writes=[t])
                    dve(lambda e, t=t, fi=fi: e.tensor_tensor(B3(fi).r, t.f, t.f, ALU.mult), reads=[t], writes=[B3(fi)])
                for j in range(16):
                    w2 = ws.get()
                    bk = pb()
                    mm_group(bk, 512, [(w2.sub(k * 128, (k + 1) * 128), B3(k)) for k in range(16)],
                             reads=[w2] + [B3(k) for k in range(16)])
                    dve(lambda e, j=j, bk=bk: e.tensor_tensor(B1(j).f, ps[:, bk, :], B1(j).f, ALU.add),
                        reads=[PSK(bk), B1(j)], writes=[B1(j)])
                    if qg == 3:
                        ln_stats(j, B1(j))

            if rnd == 0:
                tap(5, B1)
            rstd, nmr = ln_finish()
            for j in range(16):
                t = tp_s.get(T)
                dve(lambda e, t=t, j=j: e.tensor_tensor(t.f, B1(j).f, rstd.f, ALU.mult), reads=[B1(j), rstd], writes=[t])
                dve(lambda e, t=t: e.tensor_tensor(t.f, t.f, nmr.f, ALU.add), reads=[t, nmr], writes=[t])
                o = tp_s.get(T)
                act(o.f, t.f, AF.Identity, bias=cvc(B2_ + j), scale=cvc(G2_ + j), reads=[t, CVK], writes=[o])
                last_out_tok = P.op("sp", lambda e, o=o, j=j, rnd=rnd: e.dma_start(out=outT[rnd][:, j * T:(j + 1) * T], in_=o.f),
                                    reads=o.keys(), writes=[("dr", "out", rnd, j)], sem=f"st{j}")
        for j in range(16):
            P.final_wait("sp", (f"st{j}", P.cnt[f"st{j}"]))
            if DEBUG:
                P.final_wait("sp", (f"dbg{j}", P.cnt[f"dbg{j}"]))

        sems = {}
        for k in P.cnt:
            sems[k] = es.enter_context(nc.semaphore(str(k)))
        block = es.enter_context(nc.Block())
        engmap = {"pe": "tensor", "act": "scalar", "dve": "vector", "pool": "gpsimd", "sp": "sync"}

        def make(engname):
            items = P.q[engname]

            def body(e):
                for it in items:
                    if it[0] == "wait":
                        e.wait_ge(sems[it[1]], it[2])
                    else:
                        ins = it[1](e)
                        ins.then_inc(sems[it[2]], it[3])
            return body
        for en, attr in engmap.items():
            getattr(block, attr)(make(en))
    return nc


_CACHE = {}


def _tile_cols(w, ncol=128):
    K, N = w.shape
    kc = K // 128
    a = w.reshape(kc, 128, N // ncol, ncol).transpose(2, 1, 0, 3)
    return np.ascontiguousarray(a).reshape(N // ncol, 128, kc * ncol)


def kernel(x, w_in, pool_w, pool_scale, conv_w, conv_b, lru_wa, lru_ba, lru_wx, lru_bx,
           lru_lambda, w_pool_up, w_lru_up, w_out, b_out, ln1_g, ln1_b,
           w_ff1, b_ff1, w_ff2, b_ff2, ln2_g, ln2_b):
    f = np.float32
    x = np.asarray(x, f)
    if "nc" not in _CACHE:
        _CACHE["nc"] = build_program()
    nc = _CACHE["nc"]

    def vec(v):
        return np.asarray(v, f).reshape(-1, 128).T

    w_in_t = _tile_cols(np.asarray(w_in[0], f))
    pool_w_t = np.stack([
        np.ascontiguousarray(np.asarray(pool_w[0, g], f).reshape(4, 128, 512).transpose(1, 0, 2)).reshape(128, 2048)
        for g in range(4)])
    lw = np.zeros((8, 128, 2, 2, 2, 256), f)
    for d in range(2):
        for gate, wsrc in ((0, lru_wa), (1, lru_wx)):
            for h in range(8):
                m = np.asarray(wsrc[0, d, h], f)
                lw[h, :, d, gate] = m.reshape(2, 128, 256).transpose(1, 0, 2)
    lru_w_t = lw.reshape(8, 128, 2048)
    wpu_t = _tile_cols(np.asarray(w_pool_up[0], f))
    wlu_t = _tile_cols(np.asarray(w_lru_up[0], f))
    wout_t = _tile_cols(np.asarray(w_out[0], f))
    w1_t = _tile_cols(np.asarray(w_ff1[0], f))
    w2 = np.asarray(w_ff2[0], f)
    w2_t = np.concatenate([_tile_cols(w2[qg * 2048:(qg + 1) * 2048]) for qg in range(4)], axis=0)

    cvec = np.zeros((128, NV), f)
    cvec[:, PS_:PS_ + 16] = vec(pool_scale[0])
    for k in range(4):
        cvec[:, CW_ + k * 16:CW_ + (k + 1) * 16] = vec(conv_w[0, k])
    cvec[:, CB_:CB_ + 16] = vec(conv_b[0])
    for d in range(2):
        cvec[:, BA_ + d * 16:BA_ + (d + 1) * 16] = vec(lru_ba[0, d])
        cvec[:, BX_ + d * 16:BX_ + (d + 1) * 16] = vec(lru_bx[0, d])
        cvec[:, LAM_ + d * 16:LAM_ + (d + 1) * 16] = vec(lru_lambda[0, d])
    cvec[:, BO_:BO_ + 16] = vec(b_out[0])
    cvec[:, G1_:G1_ + 16] = vec(ln1_g[0])
    cvec[:, B1_:B1_ + 16] = vec(ln1_b[0])
    cvec[:, BF1_:BF1_ + 64] = vec(b_ff1[0])
    cvec[:, BF2_:BF2_ + 16] = vec(b_ff2[0])
    cvec[:, G2_:G2_ + 16] = vec(ln2_g[0])
    cvec[:, B2_:B2_ + 16] = vec(ln2_b[0])

    shared = dict(w_in_t=w_in_t, pool_w_t=pool_w_t, lru_w_t=lru_w_t, wpu_t=wpu_t, wlu_t=wlu_t,
                  wout_t=wout_t, w1_t=w1_t, w2_t=w2_t, cvec=cvec)

    in_maps = []
    for c in range(NCORE):
        b, q = c // 4, c % 4
        xb = x[b]
        xpad = np.zeros((S + 16, D), f)
        xpad[8:8 + S] = xb
        xm = np.zeros((2, 128, 16 * TH), f)
        msk = np.zeros((128, 16), f)
        rc = np.zeros((128, 128), f)
        for r in range(2):
            gb = 2 * q + r
            t0 = gb * T
            seg = xpad[t0:t0 + TH]
            xm[r] = seg.T.reshape(16, 128, TH).transpose(1, 0, 2).reshape(128, 16 * TH)
            msk[:, r * 8 + gb] = 1.0
            for g in range(4):
                w = 2 << g
                for e in range(16):
                    t = t0 + (e if e < 8 else T - 16 + e)
                    lo, hi = max(t - w // 2, 0), min(t + w // 2, S)
                    rc[:, (r * 4 + g) * 16 + e] = 1.0 / float(hi - lo)
        items = [(blk, 0) for blk in range(0, 2 * q)] + [(blk, 1) for blk in range(7, 2 * q, -1)]
        assert len(items) == NIT
        xs = np.zeros((NIT, 128, 16 * TL), f)
        lru_wi = np.zeros((NIT * 8, 128, 1024), f)
        itm = np.zeros((128, NI), f)
        for it, (blk, dr) in enumerate(items):
            t0 = blk * T
            if dr == 0:
                seg = xpad[t0 + 6:t0 + 6 + TL]
            else:
                seg = xpad[t0 + 5:t0 + 5 + TL][::-1]
            xs[it] = seg.T.reshape(16, 128, TL).transpose(1, 0, 2).reshape(128, 16 * TL)
            for k in range(4):
                kk = k if dr == 0 else 3 - k
                itm[:, CWI_ + it * 64 + k * 16:CWI_ + it * 64 + (k + 1) * 16] = vec(conv_w[0, kk])
            itm[:, BAI_ + it * 16:BAI_ + (it + 1) * 16] = vec(lru_ba[0, dr])
            itm[:, BXI_ + it * 16:BXI_ + (it + 1) * 16] = vec(lru_bx[0, dr])
            itm[:, LAMI_ + it * 16:LAMI_ + (it + 1) * 16] = vec(lru_lambda[0, dr])
            itm[:, KEEP_ + it] = 0.0 if (it == 0 or it == 2 * q) else 1.0
            itm[:, SELFA_ + it] = 1.0 if (q > 0 and it == 2 * q - 1) else 0.0
            itm[:, SELBB_ + it] = 1.0 if (it == 5 and q <= 2) else 0.0
            for h in range(8):
                for gate, wsrc in ((0, lru_wa), (1, lru_wx)):
                    mm = np.asarray(wsrc[0, dr, h], f)
                    lru_wi[it * 8 + h, :, gate * 512:(gate + 1) * 512] = mm.reshape(2, 128, 256).transpose(1, 0, 2).reshape(128, 512)
        m = dict(shared)
        m.update(xm=xm, xs=xs, rc=rc, msk=msk, lru_wi=lru_wi, itm=itm)
        in_maps.append(m)

    res = run_bass_kernel_spmd(nc, in_maps, core_ids=list(range(NCORE)))
    if DEBUG:
        _CACHE["dbg"] = [np.asarray(res.results[c]["dbg"]) for c in range(NCORE)]
    out = np.zeros((2, S, D), f)
    for c in range(NCORE):
        b, q = c // 4, c % 4
        o = np.asarray(res.results[c]["outT"], f).reshape(2, 128, 16, T)
        for r in range(2):
            gb = 2 * q + r
            out[b, gb * T:(gb + 1) * T, :] = o[r].transpose(2, 1, 0).reshape(T, D)
    return out
```
